# Optimizing a Trainium2 kernel written in Bass

```python
import math
import jax, jax.numpy as jnp
from jax import lax
import numpy as np

D_MODEL = 1024
BATCH = 8
SEQ = 2048
DEPTH = 1

NSA_HEADS = 8
NSA_KV_GROUPS = 2
NSA_REP = NSA_HEADS // NSA_KV_GROUPS
NSA_HEAD_DIM = 64
CMP_LEN = 32
CMP_STRIDE = 16
SLC_LEN = 64
SLC_TOPK = 8
WIN_LEN = 512
Q_BLK = 128
SEL_Q_BLK = 64
FORCE_SCORE = 1.0e4
NEG = -1.0e30
GDN_HEADS = 4
GDN_HEAD_DIM = 128
GDN_CONV = 4
GDN_CHUNK = 64
FFN_DIM = 2816
FFN_CONV = 3
DEEPNORM_ALPHA = (2.0 * DEPTH) ** 0.25
DEEPNORM_BETA = (8.0 * DEPTH) ** -0.25
LN_EPS = 1e-5
RMS_EPS = 1e-6

NSA_Q_W = NSA_HEADS * NSA_HEAD_DIM
NSA_KV_W = NSA_KV_GROUPS * NSA_HEAD_DIM
GDN_W = GDN_HEADS * GDN_HEAD_DIM
IN_WIDTHS = (NSA_Q_W, NSA_KV_W, NSA_KV_W, NSA_KV_W, NSA_KV_W, NSA_KV_W, NSA_KV_W,
             3 * NSA_HEADS, 3 * GDN_W, GDN_HEADS, GDN_HEADS, GDN_W, 2 * D_MODEL)
IN_WIDTH = sum(IN_WIDTHS)

kernel_name = "nsa_gdn_gated_hybrid_deepnorm"


def _split_in(h):
    offs = np.cumsum((0,) + IN_WIDTHS)
    return [h[..., int(offs[i]):int(offs[i + 1])] for i in range(len(IN_WIDTHS))]


def layer_norm(x, g, b):
    xf = x.astype(jnp.float32)
    mu = jnp.mean(xf, -1, keepdims=True)
    var = jnp.mean(jnp.square(xf - mu), -1, keepdims=True)
    return ((xf - mu) * lax.rsqrt(var + LN_EPS) * g + b).astype(x.dtype)


def rms_norm(x, w):
    xf = x.astype(jnp.float32)
    return xf * lax.rsqrt(jnp.mean(jnp.square(xf), -1, keepdims=True) + RMS_EPS) * w


def l2_norm(x):
    return x * lax.rsqrt(jnp.sum(jnp.square(x), -1, keepdims=True) + RMS_EPS)


def causal_dwconv(x, w):
    K, S = w.shape[0], x.shape[1]
    xp = jnp.pad(x, ((0, 0), (K - 1, 0), (0, 0)))
    out = xp[:, 0:S] * w[0]
    for j in range(1, K):
        out = out + xp[:, j:j + S] * w[j]
    return out


def alibi_slopes(n):
    return jnp.asarray([2.0 ** (-8.0 * (h + 1) / n) for h in range(n)], jnp.float32)


def nsa_attention(q, kc, vc, ks, vs, kw, vw, gates, cmp_pos, cmp_w1, cmp_w2):
    B, S, _ = q.shape
    G, R, DH = NSA_KV_GROUPS, NSA_REP, NSA_HEAD_DIM
    dt = q.dtype
    f32 = jnp.float32
    scale = DH ** -0.5
    q = q.reshape(B, S, G, R, DH).transpose(0, 2, 3, 1, 4)
    heads = lambda t: t.reshape(B, S, G, DH).transpose(0, 2, 1, 3)
    kc, vc, ks, vs, kw, vw = (heads(t) for t in (kc, vc, ks, vs, kw, vw))
    slopes = alibi_slopes(NSA_HEADS).reshape(G, R)
    t_pos = jnp.arange(S)

    n_cmp = (S - CMP_LEN) // CMP_STRIDE + 1
    cmp_idx = np.arange(n_cmp)[:, None] * CMP_STRIDE + np.arange(CMP_LEN)[None, :]

    def compress(t, i):
        blk = t[:, :, cmp_idx] + cmp_pos[i]
        blk = blk.reshape(B, G, n_cmp, CMP_LEN * DH)
        return jax.nn.gelu(blk @ cmp_w1[i]) @ cmp_w2[i]

    k_cmp, v_cmp = compress(kc, 0), compress(vc, 1)
    cmp_end = jnp.asarray(cmp_idx[:, -1])
    dist_c = (t_pos[:, None] - cmp_end[None, :]).astype(f32)
    valid_c = dist_c >= 0
    s_c = jnp.einsum('bgrtd,bgnd->bgrtn', q, k_cmp).astype(f32) * scale
    s_c = jnp.where(valid_c, s_c - slopes[None, :, :, None, None] * dist_c, NEG)
    p_cmp = jax.nn.softmax(s_c, axis=-1) * valid_c
    o_cmp = jnp.einsum('bgrtn,bgnd->bgrtd', p_cmp.astype(dt), v_cmp)

    n_slc = S // SLC_LEN
    c_start = np.arange(n_cmp)[:, None] * CMP_STRIDE
    s_start = np.arange(n_slc)[None, :] * SLC_LEN
    overlap = ((c_start < s_start + SLC_LEN) & (c_start + CMP_LEN > s_start)).astype(np.float32)
    score = jnp.einsum('bgrtn,nj->bgtj', p_cmp, jnp.asarray(overlap))
    blk = jnp.arange(n_slc)[None, :]
    cur = (t_pos // SLC_LEN)[:, None]
    forced = (blk == 0) | (blk == cur) | (blk == cur - 1)
    score = jnp.where(forced, FORCE_SCORE, jnp.where(blk <= cur, score, -1.0))
    k_sel = min(SLC_TOPK, n_slc)
    _, idx = lax.top_k(score, k_sel)

    ks_blk = ks.reshape(B, G, n_slc, SLC_LEN, DH)
    vs_blk = vs.reshape(B, G, n_slc, SLC_LEN, DH)
    nq = S // SEL_Q_BLK
    q_ch = jnp.moveaxis(q.reshape(B, G, R, nq, SEL_Q_BLK, DH), 3, 0)
    i_ch = jnp.moveaxis(idx.reshape(B, G, nq, SEL_Q_BLK, k_sel), 2, 0)
    t_ch = t_pos.reshape(nq, SEL_Q_BLK)
    bi = jnp.arange(B)[:, None, None, None]
    gi = jnp.arange(G)[None, :, None, None]
    offs = jnp.arange(SLC_LEN)
    n_key = k_sel * SLC_LEN

    def sel_chunk(args):
        qc, ic, tc = args
        kg = ks_blk[bi, gi, ic].reshape(B, G, SEL_Q_BLK, n_key, DH)
        vg = vs_blk[bi, gi, ic].reshape(B, G, SEL_Q_BLK, n_key, DH)
        kpos = (ic[..., None] * SLC_LEN + offs).reshape(B, G, SEL_Q_BLK, n_key)
        dist = (tc[:, None] - kpos).astype(f32)
        s = jnp.einsum('bgrtd,bgtsd->bgrts', qc, kg).astype(f32) * scale
        s = jnp.where((dist >= 0)[:, :, None],
                      s - slopes[None, :, :, None, None] * dist[:, :, None], NEG)
        p = jax.nn.softmax(s, axis=-1)
        return jnp.einsum('bgrts,bgtsd->bgrtd', p.astype(dt), vg)

    o_slc = lax.map(sel_chunk, (q_ch, i_ch, t_ch))
    o_slc = jnp.moveaxis(o_slc, 0, 3).reshape(B, G, R, S, DH)

    nb = S // Q_BLK
    nw = WIN_LEN // Q_BLK

    def band(t):
        tb = jnp.pad(t.reshape(B, G, nb, Q_BLK, DH), ((0, 0), (0, 0), (nw, 0), (0, 0), (0, 0)))
        return jnp.concatenate([tb[:, :, j:j + nb] for j in range(nw + 1)], axis=3)

    kwb, vwb = band(kw), band(vw)
    kpos_w = (jnp.arange(nb)[:, None] - nw) * Q_BLK + jnp.arange((nw + 1) * Q_BLK)[None, :]
    tq = jnp.arange(nb)[:, None] * Q_BLK + jnp.arange(Q_BLK)[None, :]
    dist_w = tq[:, :, None] - kpos_w[:, None, :]
    valid_w = (dist_w >= 0) & (dist_w < WIN_LEN) & (kpos_w[:, None, :] >= 0)
    qb = q.reshape(B, G, R, nb, Q_BLK, DH)
    s_w = jnp.einsum('bgrcqd,bgckd->bgrcqk', qb, kwb).astype(f32) * scale
    s_w = jnp.where(valid_w, s_w - slopes[None, :, :, None, None, None] * dist_w.astype(f32), NEG)
    p_w = jax.nn.softmax(s_w, axis=-1)
    o_win = jnp.einsum('bgrcqk,bgckd->bgrcqd', p_w.astype(dt), vwb).reshape(B, G, R, S, DH)

    g = jax.nn.sigmoid(gates).reshape(B, S, 3, G, R).transpose(2, 0, 3, 4, 1)[..., None]
    o = g[0] * o_cmp + g[1] * o_slc + g[2] * o_win
    return o.transpose(0, 3, 1, 2, 4).reshape(B, S, NSA_Q_W)


def gated_deltanet(qkv, beta_raw, decay_raw, gate_raw, conv_w, a_log, dt_bias, norm_w):
    B, S, _ = qkv.shape
    H, Dh, C = GDN_HEADS, GDN_HEAD_DIM, GDN_CHUNK
    dt = qkv.dtype
    f32 = jnp.float32
    qkv = jax.nn.silu(causal_dwconv(qkv, conv_w)).astype(f32)
    q, k, v = jnp.split(qkv, 3, axis=-1)
    heads = lambda t: t.reshape(B, S, H, Dh).transpose(0, 2, 1, 3)
    q = l2_norm(heads(q)) * (Dh ** -0.5)
    k = l2_norm(heads(k))
    v = heads(v)
    beta = jax.nn.sigmoid(beta_raw.astype(f32)).transpose(0, 2, 1)
    g = (-jnp.exp(a_log.astype(f32))[None, :, None]
         * jax.nn.softplus(decay_raw.astype(f32) + dt_bias.astype(f32)).transpose(0, 2, 1))
    N = S // C
    q, k, v = (t.reshape(B, H, N, C, Dh) for t in (q, k, v))
    beta, g = beta.reshape(B, H, N, C), g.reshape(B, H, N, C)
    gcum = jnp.cumsum(g, axis=-1)
    causal = jnp.tril(jnp.ones((C, C), bool))
    strict = jnp.tril(jnp.ones((C, C), bool), -1)
    decay = jnp.exp(jnp.where(causal, gcum[..., :, None] - gcum[..., None, :], -jnp.inf))
    kb = k * beta[..., None]
    m = jnp.where(strict, jnp.einsum('bhncd,bhnsd->bhncs', kb, k) * decay, 0.0)
    eye = jnp.eye(C, dtype=f32)
    T = lax.linalg.triangular_solve(eye + m, jnp.broadcast_to(eye, m.shape),
                                    left_side=True, lower=True, unit_diagonal=True)
    u = T @ (v * beta[..., None])
    w = T @ (kb * jnp.exp(gcum)[..., None])
    a_qk = jnp.einsum('bhncd,bhnsd->bhncs', q, k) * decay

    def step(state, xs):
        qn, kn, un, wn, an, gn = xs
        v_new = un - wn @ state
        o = (qn * jnp.exp(gn)[..., None]) @ state + an @ v_new
        g_last = gn[..., -1:]
        state = (state * jnp.exp(g_last)[..., None]
                 + jnp.einsum('bhcd,bhce->bhde', kn * jnp.exp(g_last - gn)[..., None], v_new))
        return state, o

    xs = tuple(jnp.moveaxis(t, 2, 0) for t in (q, k, u, w, a_qk, gcum))
    _, o = lax.scan(step, jnp.zeros((B, H, Dh, Dh), f32), xs)
    o = jnp.moveaxis(o, 0, 2).reshape(B, H, S, Dh).transpose(0, 2, 1, 3)
    o = rms_norm(o, norm_w) * jax.nn.silu(gate_raw.astype(f32).reshape(B, S, H, Dh))
    return o.reshape(B, S, GDN_W).astype(dt)


def conv_ffn(x, w_up, conv_w, w_down):
    h = causal_dwconv(x @ w_up, conv_w)
    gate, val = jnp.split(h, 2, axis=-1)
    return (jax.nn.silu(gate) * val) @ w_down


def setup_inputs(seed: int = 0) -> dict:
    key = jax.random.key(seed)
    ks = jax.random.split(key, 24)
    f32 = jnp.float32
    nrm = lambda k, shape, fan_in, gain=1.0: jax.random.normal(k, shape, f32) * (gain * fan_in ** -0.5)
    offs = np.cumsum((0,) + IN_WIDTHS)
    col_scale = np.ones(IN_WIDTH, np.float32)
    for i in (2, 4, 6):
        col_scale[offs[i]:offs[i + 1]] = DEEPNORM_BETA
    col_scale[offs[8] + 2 * GDN_W:offs[9]] = DEEPNORM_BETA
    dtv = jnp.exp(jax.random.uniform(ks[8], (DEPTH, GDN_HEADS), f32,
                                     minval=math.log(1e-3), maxval=math.log(1e-1)))
    return {
        "x": jax.random.normal(ks[0], (BATCH, SEQ, D_MODEL), f32),
        "w_in": nrm(ks[1], (DEPTH, D_MODEL, IN_WIDTH), D_MODEL) * jnp.asarray(col_scale),
        "nsa_cmp_pos": 0.02 * jax.random.normal(ks[2], (DEPTH, 2, CMP_LEN, NSA_HEAD_DIM), f32),
        "nsa_cmp_w1": nrm(ks[3], (DEPTH, 2, CMP_LEN * NSA_HEAD_DIM, NSA_HEAD_DIM), CMP_LEN * NSA_HEAD_DIM),
        "nsa_cmp_w2": nrm(ks[4], (DEPTH, 2, NSA_HEAD_DIM, NSA_HEAD_DIM), NSA_HEAD_DIM),
        "w_nsa_out": nrm(ks[5], (DEPTH, NSA_Q_W, D_MODEL), NSA_Q_W, DEEPNORM_BETA),
        "gdn_conv_w": nrm(ks[6], (DEPTH, GDN_CONV, 3 * GDN_W), GDN_CONV),
        "gdn_a_log": jnp.log(jax.random.uniform(ks[7], (DEPTH, GDN_HEADS), f32, minval=1.0, maxval=16.0)),
        "gdn_dt_bias": dtv + jnp.log(-jnp.expm1(-dtv)),
        "gdn_norm_w": 1.0 + 0.02 * jax.random.normal(ks[9], (DEPTH, GDN_HEAD_DIM), f32),
        "w_gdn_out": nrm(ks[10], (DEPTH, GDN_W, D_MODEL), GDN_W, DEEPNORM_BETA),
        "w_o": nrm(ks[11], (DEPTH, D_MODEL, D_MODEL), D_MODEL, DEEPNORM_BETA),
        "ln1_g": 1.0 + 0.02 * jax.random.normal(ks[12], (DEPTH, D_MODEL), f32),
        "ln1_b": 0.02 * jax.random.normal(ks[13], (DEPTH, D_MODEL), f32),
        "ffn_w_up": nrm(ks[14], (DEPTH, D_MODEL, 2 * FFN_DIM), D_MODEL, DEEPNORM_BETA),
        "ffn_conv_w": nrm(ks[15], (DEPTH, FFN_CONV, 2 * FFN_DIM), FFN_CONV),
        "ffn_w_down": nrm(ks[16], (DEPTH, FFN_DIM, D_MODEL), FFN_DIM, DEEPNORM_BETA),
        "ln2_g": 1.0 + 0.02 * jax.random.normal(ks[17], (DEPTH, D_MODEL), f32),
        "ln2_b": 0.02 * jax.random.normal(ks[18], (DEPTH, D_MODEL), f32),
    }


def reference(x, w_in, nsa_cmp_pos, nsa_cmp_w1, nsa_cmp_w2, w_nsa_out, gdn_conv_w, gdn_a_log,
              gdn_dt_bias, gdn_norm_w, w_gdn_out, w_o, ln1_g, ln1_b, ffn_w_up, ffn_conv_w,
              ffn_w_down, ln2_g, ln2_b):
    for l in range(DEPTH):
        h = x @ w_in[l]
        (nsa_q, cmp_k, cmp_v, slc_k, slc_v, win_k, win_v, nsa_gate,
         gdn_qkv, gdn_beta, gdn_decay, gdn_gate, merge_gate) = _split_in(h)
        y_a = nsa_attention(nsa_q, cmp_k, cmp_v, slc_k, slc_v, win_k, win_v, nsa_gate,
                            nsa_cmp_pos[l], nsa_cmp_w1[l], nsa_cmp_w2[l]) @ w_nsa_out[l]
        y_b = gated_deltanet(gdn_qkv, gdn_beta, gdn_decay, gdn_gate, gdn_conv_w[l],
                             gdn_a_log[l], gdn_dt_bias[l], gdn_norm_w[l]) @ w_gdn_out[l]
        gate_a, gate_b = jnp.split(merge_gate, 2, axis=-1)
        mix = (jax.nn.sigmoid(gate_a) * y_a + jax.nn.sigmoid(gate_b) * y_b) @ w_o[l]
        x = layer_norm(DEEPNORM_ALPHA * x + mix, ln1_g[l], ln1_b[l])
        f = conv_ffn(x, ffn_w_up[l], ffn_conv_w[l], ffn_w_down[l])
        x = layer_norm(DEEPNORM_ALPHA * x + f, ln2_g[l], ln2_b[l])
    return x
```

```python
import os
from contextlib import ExitStack
import numpy as np
import concourse.bass as bass
import concourse.mybir as mybir
from concourse.bass_utils import run_bass_kernel_spmd

F32 = mybir.dt.float32
BF16 = mybir.dt.bfloat16
AF = mybir.ActivationFunctionType
ALU = mybir.AluOpType
AX = mybir.AxisListType

S = 2048
D = 1024
NT = 16
IN_W = 5408
FFN = 2816
ALPHA = 2.0 ** 0.25
NEGM = -30000.0


class Buf:
    __slots__ = ("name", "w", "r")

    def __init__(self, name):
        self.name = name
        self.w = None
        self.r = {}


class Op:
    __slots__ = ("eng", "fn", "dma", "key", "idx", "deps", "sig", "sigval", "dmaval")


class FW:
    ENGS = ("pe", "act", "dve", "pool", "sp")

    def __init__(self, nc):
        self.nc = nc
        self.ops = []
        self.dma_cum = {}

    def op(self, eng, fn, reads=(), writes=(), dma=0, key=None):
        o = Op()
        o.eng, o.fn, o.dma, o.idx = eng, fn, dma, len(self.ops)
        o.sig, o.sigval, o.dmaval, o.key = False, 0, 0, None
        deps = set()
        raw = set()
        for b in reads:
            if b.w is not None:
                deps.add(b.w)
                raw.add(b.w)
        for b in writes:
            if b.w is not None:
                deps.add(b.w)
            for r in b.r.values():
                deps.add(r)
        keep = []
        for d in deps:
            if d is o:
                continue
            if not d.dma and not dma and d.eng == eng:
                if eng == "pe" or d not in raw:
                    continue
            keep.append(d)
            if not d.dma:
                d.sig = True
        keep.sort(key=lambda d: d.idx)
        o.deps = keep
        if dma:
            if key is None:
                key = writes[0].name if writes else reads[0].name
            o.key = key
            self.dma_cum[key] = self.dma_cum.get(key, 0) + 16 * dma
            o.dmaval = self.dma_cum[key]
        for b in reads:
            b.r[("dma", o.idx) if dma else eng] = o
        for b in writes:
            b.w = o
            b.r = {}
        self.ops.append(o)
        return o

    def emit(self, stack):
        nc = self.nc
        esem = {e: stack.enter_context(nc.semaphore("es_" + e)) for e in self.ENGS}
        dsem = {k: stack.enter_context(nc.semaphore("ds_%d" % i)) for i, k in enumerate(self.dma_cum)}
        cnt = {e: 0 for e in self.ENGS}
        for o in self.ops:
            if not o.dma and o.sig:
                cnt[o.eng] += 1
                o.sigval = cnt[o.eng]
        block = stack.enter_context(nc.Block())

        def run(ename, eng):
            waited = {}
            for o in self.ops:
                if o.eng != ename:
                    continue
                need = {}
                for d in o.deps:
                    if d.dma:
                        s, v = dsem[d.key], d.dmaval
                    else:
                        s, v = esem[d.eng], d.sigval
                    if need.get(s, (None, 0))[1] < v:
                        need[s] = (s, v)
                for s, v in need.values():
                    if waited.get(s, 0) < v:
                        eng.wait_ge(s, v)
                        waited[s] = v
                if o.fn is None:
                    continue
                res = o.fn(eng)
                if not isinstance(res, (list, tuple)):
                    res = [res]
                if o.dma:
                    assert len(res) == o.dma, (len(res), o.dma)
                    for ins in res:
                        ins.then_inc(dsem[o.key], 16)
                elif o.sig:
                    res[-1].then_inc(esem[ename], 1)
            if ename == "sp":
                for k, v in self.dma_cum.items():
                    eng.wait_ge(dsem[k], v)

        @block.tensor
        def _(e):
            run("pe", e)

        @block.scalar
        def _(e):
            run("act", e)

        @block.vector
        def _(e):
            run("dve", e)

        @block.gpsimd
        def _(e):
            run("pool", e)

        @block.sync
        def _(e):
            run("sp", e)


def _consts():
    c = {}
    c["ident"] = np.eye(128, dtype=np.float32)
    t = np.arange(S)
    qa = np.zeros((3, 8, S), np.float32)
    for h in range(8):
        sl = 2.0 ** (-(h + 1))
        qa[0, h] = sl
        qa[1, h] = -sl * (t % 256)
        qa[2, h] = -sl * 256.0 * ((t % 512) // 256)
    c["qaug"] = qa
    ka = np.ones((3, S), np.float32)
    ka[0] = t % 128
    c["kaug"] = ka
    kc = np.ones((3, 128), np.float32)
    kc[0] = 16.0 * np.arange(128)
    c["kcaug"] = kc
    i = np.arange(128)[:, None]
    j = np.arange(128)[None, :]
    m = np.zeros((2, 128, 128), np.float32)
    m[0] = np.where(j >= i, 0.0, NEGM)
    m[1] = np.where(j < i, 0.0, NEGM)
    c["masks"] = m
    n = np.arange(128)[:, None]
    jj = np.arange(512)[None, :]
    cm = np.zeros((4, 128, 512), np.float32)
    for qt in range(4):
        cm[qt] = np.where(qt * 512 + jj >= 16 * n + 31, 0.0, NEGM)
    c["cmask"] = cm
    e = np.zeros((32, 16, 128), np.float32)
    for kt in range(16):
        for ii in range(128):
            e[2 * kt + ii // 64, kt, ii] = 1.0
    c["emat"] = e
    ta = np.zeros((128, 16, 32), np.float32)
    tb = np.zeros((128, 16, 32), np.float32)
    for tt in range(16):
        for p in range(128):
            cur = (tt * 128 + p) // 64
            for b in range(32):
                forced = (b == 0) or (b == cur) or (b == cur - 1)
                if forced:
                    tb[p, tt, b] = 1.0e4
                elif b <= cur:
                    ta[p, tt, b] = 1.0
                else:
                    tb[p, tt, b] = -1.0
    c["topa"] = ta
    c["topb"] = tb
    ov = np.zeros((128, 33), np.float32)
    cs = np.arange(127)[:, None] * 16
    ss = np.arange(32)[None, :] * 64
    ov[:127, :32] = ((cs < ss + 64) & (cs + 32 > ss)).astype(np.float32)
    ov[:127, 32] = 1.0
    c["ovl"] = ov
    g = np.zeros((8, 128, 128), np.float32)
    r = np.arange(128)[:, None]
    cc = np.arange(128)[None, :]
    g[0] = (r <= cc)
    g[1] = (r > cc)
    g[2] = (cc <= r)
    g[3] = (cc < r)
    g[4] = (r <= cc)
    g[5] = 1.0
    g[6] = (cc < r) & ((cc // 64) == (r // 64))
    g[7] = (r >= 64) & (cc < 64)
    c["gmat"] = g
    return c


CONST = _consts()
DBG = os.environ.get("MK_DBG", "")


ARENA_WORDS = 53000


class Bld:
    def __init__(self, dbg=""):
        self.dbg = dbg
        self.nc = bass.Bass("TRN2", target_bir_lowering=False)
        self.fw = FW(self.nc)
        self.st = ExitStack()
        self.bufs = []
        self.off = 0
        self.din = {}
        self.wrot = 0

    def buf(self, name):
        b = Buf(name)
        self.bufs.append(b)
        return b

    def alloc(self, shape, dt):
        n = int(np.prod(shape))
        words = (n + 1) // 2 if dt == BF16 else n
        words = (words + 7) // 8 * 8
        a = self.arena[:, self.off:self.off + words]
        self.off += words
        assert self.off <= ARENA_WORDS, ("SBUF arena overflow", self.off)
        if dt == BF16:
            a = a.bitcast(BF16)[:, 0:n]
        else:
            a = a[:, 0:n]
        if len(shape) == 2:
            a = a.rearrange("p (a b) -> p a b", a=shape[0])
        elif len(shape) == 3:
            a = a.rearrange("p (a b c) -> p a b c", a=shape[0], b=shape[1])
        return a

    def inp(self, name, shape):
        ap = self.nc.dram_tensor(name, list(shape), F32, kind="ExternalInput").ap()
        self.din[name] = ap
        return ap

    def barrier(self):
        deps = set()
        for b in self.bufs:
            if b.w is not None:
                deps.add(b.w)
            for r in b.r.values():
                deps.add(r)
        for e in FW.ENGS:
            o = Op()
            o.eng, o.fn, o.dma, o.idx = e, None, 0, len(self.fw.ops)
            o.sig, o.sigval, o.dmaval, o.key = False, 0, 0, None
            o.deps = [d for d in deps if d.dma or d.eng != e]
            for d in o.deps:
                if not d.dma:
                    d.sig = True
            self.fw.ops.append(o)

    def bank(self):
        i = self.brot % len(self.banks_rot)
        self.brot += 1
        return self.banks_rot[i]

    def dma(self, eng, out, in_, reads, writes, key=None):
        self.fw.op(eng, lambda e: e.dma_start(out=out, in_=in_), reads=reads, writes=writes, dma=1, key=key)

    def ldw(self, src3, ncols, reads=()):
        i = self.wrot % len(self.wb)
        self.wrot += 1
        ap, b = self.wb[i]
        dst = ap[:, :, 0:ncols]
        self.fw.op("pool", lambda e: [e.dma_start(out=dst[:, 0:4, :], in_=src3[:, 0:4, :]),
                                      e.dma_start(out=dst[:, 4:8, :], in_=src3[:, 4:8, :])],
                   reads=list(reads), writes=[b], dma=2)
        return ap, b

    def load_xT(self):
        A, op = self.alloc, self.fw.op
        xT = A([8, S], BF16)
        b_xT = [self.buf("xT%d" % k) for k in range(8)]
        for k in range(8):
            op("pool", lambda e, k=k: e.dma_start(out=xT[:, k, :], in_=self.din["xT"][k * 128:(k + 1) * 128, :]),
               writes=[b_xT[k]], dma=1, key="xT%d" % k)
        self.xT, self.b_xT = xT, b_xT
        self.wb = [(A([8, 512], BF16), self.buf("wb%d" % i)) for i in range(3)]

    def evac(self, out, in_, reads, writes, scale=None):
        op = self.fw.op
        self.evn += 1
        if self.evn % 2 == 0:
            if scale is None:
                op("act", lambda e: e.copy(out, in_), reads=reads, writes=writes)
            else:
                op("act", lambda e: e.mul(out, in_, scale), reads=reads, writes=writes)
        else:
            if scale is None:
                op("dve", lambda e: e.tensor_copy(out, in_), reads=reads, writes=writes)
            else:
                op("dve", lambda e: e.tensor_scalar_mul(out, in_, scale), reads=reads, writes=writes)

    def proj_fm(self, wap, wbuf, c0, M, dst_fn, dbufs, scale=None, evac=None):
        op, xT, b_xT = self.fw.op, self.xT, self.b_xT
        for tt in range(4):
            bk, bb = self.bank()
            op("pe", lambda e, bk=bk, tt=tt: [
                e.matmul(bk[0:M, 0:512], lhsT=wap[:, k, c0:c0 + M], rhs=xT[:, k, tt * 512:(tt + 1) * 512],
                         start=(k == 0), stop=(k == 7)) for k in range(8)],
               reads=[wbuf] + b_xT, writes=[bb])
            if evac is None:
                self.evac(dst_fn(tt), bk[0:M, 0:512], [bb], dbufs, scale)
            else:
                evac(tt, bk, bb)

    def build(self):
        nc, fw = self.nc, self.fw
        st = self.st
        op = fw.op
        inp = self.inp
        inp("xT", [D, S])
        inp("x", [S, D])
        w_in = inp("w_in", [D, IN_W])
        self.w_in3 = w_in.rearrange("(k p) n -> p k n", p=128)
        self.cd = {k: inp("c_" + k, v.shape) for k, v in CONST.items()}
        inp("cmp_pos", [128, 2, 16])
        inp("cmp_w1", [2, 2048, 64])
        inp("cmp_w2", [2, 64, 64])
        inp("gdn_normw", [128, 512])
        for n in ("ln1_g", "ln1_b", "ln2_g", "ln2_b"):
            inp(n, [128, D])
        inp("ffn_convw", [128, 44, 3])
        inp("w_nsa_out", [512, D]); inp("w_gdn_out", [512, D]); inp("w_o", [D, D])
        inp("ffn_w_up", [D, 2 * FFN]); inp("ffn_w_down", [FFN, D])
        inp("gdn_convw", [128, 12, 4])
        inp("gdn_dtb", [128, 64])
        inp("gdn_alog", [128, 64])
        self.out_d = nc.dram_tensor("out", [S, D], F32, kind="ExternalOutput").ap()
        self.dbg_d = None
        if self.dbg:
            self.dbg_d = nc.dram_tensor("dbg", [S, 512], F32, kind="ExternalOutput").ap()
        self.arena = st.enter_context(nc.sbuf_tensor("arena", [128, ARENA_WORDS], F32))
        banks = []
        for i in range(8):
            t = st.enter_context(nc.psum_tensor("bank%d" % i, [128, 512], F32))
            banks.append((t, self.buf("bank%d" % i)))
        self.banks = banks
        self.banks_rot = banks[4:8]
        self.brot = 0
        self.evn = 0
        A = self.alloc
        self.ident = A([128], BF16)
        self.identf = A([128], F32)
        self.zeros = A([512], BF16)
        self.b_c = self.buf("consts")
        self.GT = A([NT, 32], F32); self.b_GT = self.buf("GT")
        self.oy = A([8, S], BF16)
        self.onsaT = self.oy[:, 0:4, :]; self.b_onsaT = self.buf("onsaT")
        self.ygT = self.oy[:, 4:8, :]; self.b_ygT = self.buf("ygT")
        op("pool", lambda e: e.dma_start(out=self.ident, in_=self.cd["ident"]), writes=[self.b_c], dma=1, key="c0")
        op("sp", lambda e: e.dma_start(out=self.identf, in_=self.cd["ident"]), writes=[self.b_c], dma=1, key="c1")
        op("dve", lambda e: e.memset(self.zeros, 0.0), writes=[self.b_c])
        mark = self.off
        self.p1_nsa()
        if self.dbg == "nsa":
            fw.emit(st)
            return
        self.barrier()
        self.off = mark
        self.p2_gdn()
        if self.dbg == "gdn":
            fw.emit(st)
            return
        self.barrier()
        self.off = mark
        self.p34()
        fw.emit(st)

    def dump_tok(self, src_fn, bufs, ncol=512):
        A, op = self.alloc, self.fw.op
        stg = A([512], F32); b_stg = self.buf("stg")
        for t in range(NT):
            op("dve", lambda e, t=t: e.tensor_copy(stg[:, 0:ncol], src_fn(t)), reads=bufs, writes=[b_stg])
            op("sp", lambda e, t=t: e.dma_start(out=self.dbg_d[t * 128:(t + 1) * 128, 0:ncol], in_=stg[:, 0:ncol]),
               reads=[b_stg], dma=1, key="dbgo")

    def p1_nsa(self):
        nc, fw, op, A = self.nc, self.fw, self.fw.op, self.alloc
        banks = self.banks
        ident, identf, zeros, b_c = self.ident, self.identf, self.zeros, self.b_c
        GT, b_GT, onsaT, b_onsaT = self.GT, self.b_GT, self.onsaT, self.b_onsaT
        cd, w_in3 = self.cd, self.w_in3
        self.load_xT()
        xT, b_xT = self.xT, self.b_xT
        QA = A([8, S], BF16); b_QA = [self.buf("QA%d" % h) for h in range(8)]
        KA = A([2, S], BF16); b_KA = [self.buf("KA%d" % i) for i in range(2)]
        off_ct = self.off
        CT = A([4, S], BF16); b_ct1 = self.buf("CT"); b_CT = [b_ct1] * 4
        self.off = off_ct
        otok = A([NT, 512], BF16); b_otok = b_ct1
        VA = A([2, NT, 65], BF16); b_VA = self.buf("VA")
        PT = [(A([512], BF16), self.buf("PT%d" % i)) for i in range(6)]
        cmask = A([4, 512], BF16)
        masks = A([2, 128], BF16)
        emat = A([16, 128], BF16)
        topa = A([NT, 32], F32)
        topb = A([NT, 32], F32)
        ovl = A([33], BF16)
        b_k = self.buf("nsa_consts")
        selT = A([S], BF16); b_selT = self.buf("selT")
        kcA = A([2, 128], BF16); b_kcA = self.buf("kcA")
        vcA = A([2, 65], BF16); b_vcA = self.buf("vcA")
        w1 = A([2, 32, 64], BF16)
        w1b = A([2, 16, 64], BF16)
        posb = A([2, 16], BF16)
        w2 = A([2, 64], BF16)
        b_cw = self.buf("cmp_w")
        gl = A([4, 128], BF16); b_gl = self.buf("gl")
        ctmp = A([6, 128], F32); b_ctmp = self.buf("ctmp")
        cbias = A([2], F32); b_cbias = self.buf("cbias")
        tacc = A([4, 4, 64], F32); b_tacc = self.buf("tacc")
        sm = A([4, 64], F32); b_sm = self.buf("sm")

        def cld(dst, src, key, eng="pool"):
            op(eng, lambda e: e.dma_start(out=dst, in_=src), writes=[b_k], dma=1, key=key)
        wk, bwk = self.ldw(w_in3[:, :, 512:768], 256)
        op("pool", lambda e: [e.dma_start(out=cmask, in_=cd["cmask"].rearrange("q p j -> p q j")),
                              e.dma_start(out=masks, in_=cd["masks"].rearrange("m p j -> p m j")),
                              e.dma_start(out=emat[0:32], in_=cd["emat"]),
                              e.dma_start(out=ovl, in_=cd["ovl"])], writes=[b_k], dma=4, key="k0")
        op("sp", lambda e: [e.dma_start(out=topa, in_=cd["topa"]), e.dma_start(out=topb, in_=cd["topb"])],
           writes=[b_k], dma=2, key="k1")
        for i in range(2):
            op("pool", lambda e, i=i: e.dma_start(out=KA[64:67, i, :], in_=cd["kaug"]),
               writes=[b_KA[i]], dma=1, key="ka%d" % i)
        for h in range(8):
            op("pool", lambda e, h=h: e.dma_start(out=QA[64:67, h, :], in_=cd["qaug"][:, h, :]),
               writes=[b_QA[h]], dma=1, key="qa%d" % h)
        for g in range(2):
            op("pool", lambda e, g=g: e.dma_start(out=kcA[64:67, g, :], in_=cd["kcaug"]),
               writes=[b_kcA], dma=1, key="kc%d" % g)
        w1_d, w2_d, pos_d = self.din["cmp_w1"], self.din["cmp_w2"], self.din["cmp_pos"]
        op("pool", lambda e: [e.dma_start(out=w1[0:64], in_=w1_d.rearrange("i (l d) h -> d i l h", d=64)),
                              e.dma_start(out=w1b, in_=w1_d.rearrange("i (c p) h -> p i c h", p=128)),
                              e.dma_start(out=posb, in_=pos_d),
                              e.dma_start(out=w2[0:64], in_=w2_d.rearrange("i h d -> h i d"))],
           writes=[b_cw], dma=4, key="cw0")
        op("dve", lambda e: e.memset(VA[:, :, :, 64:65], 1.0), writes=[b_VA])
        op("dve", lambda e: e.memset(vcA[:, :, 64:65], 1.0), writes=[b_vcA])

        for i in range(4):
            self.proj_fm(wk, bwk, i * 64, 64, lambda tt, i=i: CT[0:64, i, tt * 512:(tt + 1) * 512], [b_CT[i]])

        bkc, bbc = banks[4]
        op("pe", lambda e: [e.matmul(bkc[0:64, i:i + 1], lhsT=w1b[:, i, c, :], rhs=posb[:, i, c:c + 1],
                                     start=(c == 0), stop=(c == 15)) for i in range(2) for c in range(16)],
           reads=[b_cw], writes=[bbc])
        op("dve", lambda e: e.tensor_copy(cbias[0:64, :], bkc[0:64, 0:2]), reads=[bbc], writes=[b_cbias])
        bk2, bb2 = banks[5]
        for ci in range(4):
            i = ci // 2
            op("pe", lambda e, ci=ci, i=i: [
                e.matmul(bk2[0:64, ci * 128:ci * 128 + 127], lhsT=w1[0:64, i, l, :],
                         rhs=CT[0:64, ci, l:l + 16 * 126 + 1:16], start=(l == 0), stop=(l == 31)) for l in range(32)],
               reads=[b_cw, b_CT[ci]], writes=[bb2])
        for ci in range(4):
            i = ci // 2
            uu = ctmp[0:64, ci, 0:127]
            op("act", lambda e, ci=ci, i=i, uu=uu: e.activation(uu, bk2[0:64, ci * 128:ci * 128 + 127], AF.Identity,
                                                                bias=cbias[0:64, i:i + 1]),
               reads=[bb2, b_cbias], writes=[b_ctmp])
            a2 = ctmp[0:64, 4, 0:127]
            a3 = ctmp[0:64, 5, 0:127]
            op("dve", lambda e, uu=uu, a2=a2: e.tensor_tensor(a2, uu, uu, op=ALU.mult), reads=[b_ctmp], writes=[b_ctmp])
            op("dve", lambda e, a2=a2: e.tensor_scalar(a2, a2, 0.044715, 1.0, op0=ALU.mult, op1=ALU.add),
               reads=[b_ctmp], writes=[b_ctmp])
            op("dve", lambda e, uu=uu, a2=a2: e.tensor_tensor(a2, a2, uu, op=ALU.mult), reads=[b_ctmp], writes=[b_ctmp])
            op("act", lambda e, a2=a2, a3=a3: e.activation(a3, a2, AF.Sigmoid, scale=1.5957691216057308),
               reads=[b_ctmp], writes=[b_ctmp])
            op("dve", lambda e, ci=ci, uu=uu, a3=a3: e.tensor_tensor(gl[0:64, ci, 0:127], uu, a3, op=ALU.mult),
               reads=[b_ctmp], writes=[b_gl])
        bk3, bb3 = banks[6]
        for g in range(2):
            op("pe", lambda e, g=g: e.matmul(bk3[0:64, g * 128:g * 128 + 127], lhsT=w2[0:64, 0, :], rhs=gl[0:64, g, 0:127],
                                             start=True, stop=True), reads=[b_cw, b_gl], writes=[bb3])
            op("dve", lambda e, g=g: e.tensor_copy(kcA[0:64, g, 0:127], bk3[0:64, g * 128:g * 128 + 127]),
               reads=[bb3], writes=[b_kcA])
            op("pe", lambda e, g=g: e.matmul(bk3[0:127, 256 + g * 64:256 + (g + 1) * 64], lhsT=gl[0:64, 2 + g, 0:127],
                                             rhs=w2[0:64, 1, :], start=True, stop=True), reads=[b_cw, b_gl], writes=[bb3])
            op("dve", lambda e, g=g: e.tensor_copy(vcA[0:127, g, 0:64], bk3[0:127, 256 + g * 64:256 + (g + 1) * 64]),
               reads=[bb3], writes=[b_vcA])

        STOP = int(os.environ.get("MK_STOP", "0"))
        if STOP == 1:
            return
        if STOP == 6:
            stg = A([512], F32); b_stg = self.buf("stg")
            for i6, (src, np_, nc_) in enumerate([(kcA[0:67].rearrange("p a b -> p (a b)"), 67, 256),
                                                (vcA[0:127].rearrange("p a b -> p (a b)"), 127, 130),
                                                (gl[0:64].rearrange("p a b -> p (a b)"), 64, 512)]):
                op("dve", lambda e, src=src, np_=np_, nc_=nc_: e.tensor_copy(stg[0:np_, 0:nc_], src),
                   reads=[b_kcA, b_vcA, b_gl], writes=[b_stg])
                op("sp", lambda e, i6=i6, np_=np_, nc_=nc_: e.dma_start(out=self.dbg_d[i6 * 128:i6 * 128 + np_, 0:nc_], in_=stg[0:np_, 0:nc_]),
                   reads=[b_stg], dma=1, key="dbgo")
            return
        acc = banks[0:4]
        scb = banks[4]
        sbanks = banks[5:8]
        rot = {"s": 0, "p": 0}
        slopes = [2.0 ** (-(h + 1)) for h in range(8)]

        def zero_bank(bk, bb, ncol):
            op("pe", lambda e: e.matmul(bk[:, 0:ncol], lhsT=zeros[:, 0:128], rhs=zeros[:, 0:ncol], start=True, stop=True),
               reads=[b_c], writes=[bb])

        def make_items(g, qt, br):
            q0 = qt * 512
            items = []
            for hl in range(4):
                h = g * 4 + hl
                if br == 0:
                    items.append(dict(h=h, hl=hl, kind="cmp", K=127, c0=0, c1=512, bias=-slopes[h] * (q0 - 31)))
                elif br == 2:
                    for m in range(8):
                        k0 = q0 - 512 + 128 * m
                        if k0 < 0:
                            continue
                        if m < 4:
                            c0, c1, mk, mc = 0, 128 * (m + 1), 1, 128 * m
                        else:
                            c0, c1, mk, mc = 128 * (m - 4), 512, 0, 128 * (m - 4)
                        items.append(dict(h=h, hl=hl, kind="win", K=128, c0=c0, c1=c1, kt=k0 // 128, mk=mk, mc=mc,
                                          bias=-slopes[h] * (q0 - k0)))
                else:
                    for kt in range(0, (q0 + 512) // 128):
                        k0 = kt * 128
                        if k0 < q0:
                            c0, c1, mk, mc = 0, 512, None, 0
                        else:
                            mm = (k0 - q0) // 128
                            c0, c1, mk, mc = 128 * mm, 512, 0, 128 * mm
                        items.append(dict(h=h, hl=hl, kind="slc", K=128, c0=c0, c1=c1, kt=kt, mk=mk, mc=mc,
                                          bias=-slopes[h] * (q0 - k0)))
            return items

        def emit_S(g, qt, it):
            q0 = qt * 512
            bk, bb = sbanks[rot["s"] % 3]
            rot["s"] += 1
            pt, pb = PT[rot["p"] % 6]
            rot["p"] += 1
            it["pt"], it["pb"] = pt, pb
            hl, K, c0, c1 = it["hl"], it["K"], it["c0"], it["c1"]
            kind = it["kind"]

            def pe_fn(e):
                ins = []
                if kind == "cmp":
                    ins.append(e.matmul(bk[0:127, 0:512], lhsT=kcA[0:67, g, 0:127], rhs=QA[0:67, it["h"], q0:q0 + 512],
                                        start=True, stop=False))
                    ins.append(e.matmul(bk[0:127, 0:512], lhsT=ident[0:127, 0:127], rhs=cmask[0:127, qt, :],
                                        start=False, stop=True))
                else:
                    ki = 0 if kind == "win" else 1
                    kt = it["kt"]
                    more = (it["mk"] is not None) or kind == "slc"
                    ins.append(e.matmul(bk[:, c0:c1], lhsT=KA[0:67, ki, kt * 128:(kt + 1) * 128],
                                        rhs=QA[0:67, it["h"], q0 + c0:q0 + c1], start=True, stop=not more))
                    if kind == "slc":
                        ins.append(e.matmul(bk[:, c0:c1], lhsT=emat[0:32, kt, :], rhs=selT[0:32, q0 + c0:q0 + c1],
                                            start=False, stop=it["mk"] is None))
                    if it["mk"] is not None:
                        mc = it["mc"]
                        ins.append(e.matmul(bk[:, mc:mc + 128], lhsT=ident[:, :], rhs=masks[:, it["mk"], :],
                                            start=False, stop=True))
                return ins
            rd = [b_QA[it["h"]], b_k, b_c]
            if kind == "cmp":
                rd.append(b_kcA)
            elif kind == "win":
                rd.append(b_KA[0])
            else:
                rd += [b_KA[1], b_selT]
            op("pe", pe_fn, reads=rd, writes=[bb])
            op("act", lambda e: e.activation(pt[0:K, c0:c1], bk[0:K, c0:c1], AF.Exp, bias=float(it["bias"])),
               reads=[bb], writes=[pb])

        def emit_PV(g, qt, it):
            pt, pb = it["pt"], it["pb"]
            K, c0, c1, hl, kind = it["K"], it["c0"], it["c1"], it["hl"], it["kind"]

            def pe_fn(e):
                ins = []
                for qs in range(c0 // 128, c1 // 128):
                    if kind == "cmp":
                        rhs = vcA[0:127, g, :]
                    else:
                        vi = 1 if kind == "win" else 0
                        rhs = VA[:, vi, it["kt"], :]
                    ins.append(e.matmul(acc[qs][0][:, hl * 65:(hl + 1) * 65], lhsT=pt[0:K, qs * 128:(qs + 1) * 128],
                                        rhs=rhs, start=False, stop=True, skip_group_check=True))
                    if kind == "cmp":
                        ins.append(e.matmul(scb[0][:, (qs * 4 + hl) * 32:(qs * 4 + hl + 1) * 32],
                                            lhsT=pt[0:127, qs * 128:(qs + 1) * 128], rhs=ovl[0:127, 0:32],
                                            start=False, stop=True, skip_group_check=True))
                return ins
            wr = [acc[qs][1] for qs in range(c0 // 128, c1 // 128)]
            if kind == "cmp":
                wr.append(scb[1])
            op("pe", pe_fn, reads=[pb, b_VA, b_vcA, b_k], writes=wr)

        def post(g, qt, br):
            for qs in range(4):
                t = qt * 4 + qs
                ab, abuf = acc[qs]
                lm = sm[:, qs, 0:4]
                rl = sm[:, qs, 4:8]
                ff = sm[:, qs, 8:12]
                a3 = ab[:, 0:260].rearrange("p (h d) -> p h d", h=4)
                op("dve", lambda e, lm=lm, a3=a3: e.tensor_scalar_max(lm, a3[:, :, 64], 1e-30), reads=[abuf], writes=[b_sm])
                op("dve", lambda e, lm=lm, rl=rl: e.reciprocal(rl, lm), reads=[b_sm], writes=[b_sm])
                gc = br * 8 + g * 4
                op("dve", lambda e, rl=rl, ff=ff, t=t, gc=gc: e.tensor_tensor(ff, rl, GT[:, t, gc:gc + 4], op=ALU.mult),
                   reads=[b_sm, b_GT], writes=[b_sm])
                for hl in range(4):
                    if br == 0:
                        op("dve", lambda e, hl=hl, qs=qs, a3=a3, ff=ff: e.tensor_scalar_mul(tacc[:, qs, hl, :], a3[:, hl, 0:64],
                                                                                          ff[:, hl:hl + 1]),
                           reads=[abuf, b_sm], writes=[b_tacc])
                    else:
                        last = (br == 1)
                        dst = otok[:, t, (g * 4 + hl) * 64:(g * 4 + hl + 1) * 64] if last else tacc[:, qs, hl, :]
                        op("dve", lambda e, hl=hl, qs=qs, a3=a3, ff=ff, dst=dst: e.scalar_tensor_tensor(
                            dst, a3[:, hl, 0:64], ff[:, hl:hl + 1], tacc[:, qs, hl, :], op0=ALU.mult, op1=ALU.add),
                           reads=[abuf, b_sm, b_tacc], writes=[b_otok if last else b_tacc])
                if br == 0:
                    sc = sm[:, qs, 16:48]
                    s4 = scb[0][:, qs * 128:(qs + 1) * 128].rearrange("p (h b) -> p h b", h=4)
                    op("dve", lambda e, sc=sc, s4=s4, rl=rl: e.tensor_scalar_mul(sc, s4[:, 0, :], rl[:, 0:1]),
                       reads=[scb[1], b_sm], writes=[b_sm])
                    for hl in range(1, 4):
                        op("dve", lambda e, sc=sc, s4=s4, rl=rl, hl=hl: e.scalar_tensor_tensor(
                            sc, s4[:, hl, :], rl[:, hl:hl + 1], sc, op0=ALU.mult, op1=ALU.add),
                           reads=[scb[1], b_sm], writes=[b_sm])
                    op("dve", lambda e, sc=sc, t=t: e.tensor_tensor(sc, sc, topa[:, t, :], op=ALU.mult),
                       reads=[b_sm, b_k], writes=[b_sm])
                    op("dve", lambda e, sc=sc, t=t: e.tensor_tensor(sc, sc, topb[:, t, :], op=ALU.add),
                       reads=[b_sm, b_k], writes=[b_sm])
                    t8 = sm[:, qs, 48:56]
                    op("dve", lambda e, sc=sc, t8=t8: e.max(t8, sc), reads=[b_sm], writes=[b_sm])
                    op("dve", lambda e, sc=sc, t8=t8: e.tensor_scalar(sc, sc, t8[:, 7:8], -NEGM, op0=ALU.is_ge, op1=ALU.mult),
                       reads=[b_sm], writes=[b_sm])
                    sng = gl[:, qs, 0:32]
                    op("dve", lambda e, sc=sc, sng=sng: e.tensor_scalar_add(sng, sc, NEGM), reads=[b_sm], writes=[b_gl])
                    tb_, tbb = sbanks[rot["s"] % 3]
                    rot["s"] += 1
                    tv = tb_.bitcast(BF16)
                    op("pe", lambda e, tv=tv, sng=sng: e.transpose(tv[0:32, 0:128], sng, ident[:, :]),
                       reads=[b_gl, b_c], writes=[tbb])
                    op("act", lambda e, tv=tv, t=t: e.copy(selT[0:32, t * 128:(t + 1) * 128], tv[0:32, 0:128]),
                       reads=[tbb], writes=[b_selT])

        for g in range(2):
            wq, bwq = self.ldw(w_in3[:, :, g * 256:(g + 1) * 256], 256)
            for hl in range(4):
                self.proj_fm(wq, bwq, hl * 64, 64, lambda tt, hq=g * 4 + hl: QA[0:64, hq, tt * 512:(tt + 1) * 512], [b_QA[g * 4 + hl]], 0.125)
            if STOP == 21:
                return
            i = self.wrot % 3
            self.wrot += 1
            wv, bwv = self.wb[i]
            ncv = 160 if g == 0 else 128
            if int(os.environ.get("MK_VAR", "0")) == 3:
                ncv = 256
            def wload(e, g=g, wv=wv):
                ins = [e.dma_start(out=wv[:, :, 0:64], in_=w_in3[:, :, 896 + g * 64:960 + g * 64]),
                       e.dma_start(out=wv[:, :, 64:128], in_=w_in3[:, :, 1152 + g * 64:1216 + g * 64]),
                       e.dma_start(out=wv[:, :, 256:320], in_=w_in3[:, :, 1024 + g * 64:1088 + g * 64]),
                       e.dma_start(out=wv[:, :, 320:384], in_=w_in3[:, :, 768 + g * 64:832 + g * 64])]
                if g == 0:
                    ins.append(e.dma_start(out=wv[:, :, 128:152], in_=w_in3[:, :, 1280:1304]))
                    ins.append(e.dma_start(out=wv[:, :, 152:160], in_=w_in3[:, :, 2840:2848]))
                return ins
            op("pool", wload, writes=[bwv], dma=6 if g == 0 else 4)
            for i2 in range(2):
                self.proj_fm(wv, bwv, 256 + i2 * 64, 64, lambda tt, i2=i2: KA[0:64, i2, tt * 512:(tt + 1) * 512], [b_KA[i2]])
            if STOP == 22:
                return
            for t in range(int(os.environ.get("MK_NT", "16"))):
                bk, bb = self.bank()
                op("pe", lambda e, bk=bk, t=t, wv=wv, ncv=ncv: [
                    e.matmul(bk[:, 0:ncv], lhsT=xT[:, k, t * 128:(t + 1) * 128], rhs=wv[:, k, 0:ncv],
                             start=(k == 0), stop=(k == 7)) for k in range(8)],
                   reads=[bwv] + b_xT, writes=[bb])
                VAR = int(os.environ.get("MK_VAR", "0"))
                if VAR not in (1, 4):
                    self.evac(VA[:, :, t, 0:64], bk[:, 0:128].rearrange("p (v d) -> p v d", v=2), [bb], [b_VA])
                if g == 0 and VAR not in (2, 4):
                    self.evac(GT[:, t, :], bk[:, 128:160], [bb], [b_GT])
            if STOP == 23:
                return
            if g == 0:
                op("act", lambda e: e.activation(GT[:, :, 0:28], GT[:, :, 0:28], AF.Sigmoid), reads=[b_GT], writes=[b_GT])
                op("act", lambda e: e.activation(cbias[0:64, 0:1], cbias[0:64, 1:2], AF.Exp), reads=[b_cbias], writes=[b_ctmp])
            if STOP == 2 or (STOP == 4 and g == 1):
                return
            if int(os.environ.get("MK_BAR", "0")) and g == 0:
                self.barrier()
            for qt in range(4):
                for br in (0, 2, 1):
                    for qs in range(4):
                        zero_bank(acc[qs][0], acc[qs][1], 260)
                    if br == 0:
                        zero_bank(scb[0], scb[1], 512)
                    items = make_items(g, qt, br)
                    for i, it in enumerate(items):
                        emit_S(g, qt, it)
                        if i >= 3:
                            emit_PV(g, qt, items[i - 3])
                    for it in items[max(0, len(items) - 3):]:
                        emit_PV(g, qt, it)
                    post(g, qt, br)
            if STOP == 3:
                return
            if STOP == 9:
                stg = A([512], F32); b_stg = self.buf("stg")
                for t in range(2):
                    op("dve", lambda e, t=t: e.tensor_copy(stg, otok[:, t, :]), reads=[b_otok], writes=[b_stg])
                    op("sp", lambda e, t=t: e.dma_start(out=self.dbg_d[t * 128:(t + 1) * 128, :], in_=stg), reads=[b_stg], dma=1, key="dbgo")
                op("dve", lambda e: e.tensor_copy(stg[0:32], selT[0:32, 0:512]), reads=[b_selT], writes=[b_stg])
                op("sp", lambda e: e.dma_start(out=self.dbg_d[256:288, :], in_=stg[0:32]), reads=[b_stg], dma=1, key="dbgo")
                op("dve", lambda e: e.tensor_copy(stg[:, 0:128], GT.rearrange("p t c -> p (t c)")[:, 0:128]), reads=[b_GT], writes=[b_stg])
                op("sp", lambda e: e.dma_start(out=self.dbg_d[384:512, 0:128], in_=stg[:, 0:128]), reads=[b_stg], dma=1, key="dbgo")
                return
            if STOP == 7:
                stg = A([512], F32); b_stg = self.buf("stg")
                lst = [(cmask[0:127, 0, :], 127, 512), (vcA[0:127].rearrange("p a b -> p (a b)"), 127, 130),
                       (kcA[0:67, :, 0:127], 67, 254), (ovl[0:127, 0:33], 127, 33), (GT[:, 0, :], 128, 32), (selT[0:32, 0:512], 32, 512)]
                for i6, (src, np_, nc_) in enumerate(lst):
                    dst = stg[0:np_, 0:nc_]
                    if i6 == 2:
                        dst = dst.rearrange("p (a b) -> p a b", a=2)
                    op("dve", lambda e, src=src, dst=dst: e.tensor_copy(dst, src),
                       reads=[b_kcA, b_vcA, b_k, b_GT, b_selT], writes=[b_stg])
                    op("sp", lambda e, i6=i6, np_=np_, nc_=nc_: e.dma_start(out=self.dbg_d[i6 * 128:i6 * 128 + np_, 0:nc_], in_=stg[0:np_, 0:nc_]),
                       reads=[b_stg], dma=1, key="dbgo")
                return
            if STOP == 5:
                stg = A([512], F32); b_stg = self.buf("stg")
                srcs = [KA[:, 0, 0:512], KA[:, 1, 0:512]] + [QA[:, hl, 0:512] for hl in range(4)]
                for i5, src in enumerate(srcs):
                    op("dve", lambda e, src=src: e.tensor_copy(stg[0:67], src[0:67]), reads=b_KA + b_QA, writes=[b_stg])
                    op("sp", lambda e, i5=i5: e.dma_start(out=self.dbg_d[i5 * 128:i5 * 128 + 67, :], in_=stg[0:67]),
                       reads=[b_stg], dma=1, key="dbgo")
                return

        if self.dbg == "nsa":
            self.dump_tok(lambda t: otok[:, t, :], [b_otok])
            return
        for t in range(NT):
            tb_, tbb = sbanks[rot["s"] % 3]
            rot["s"] += 1
            tv = tb_.bitcast(BF16)
            op("pe", lambda e, tv=tv, t=t: [e.transpose(tv[:, c * 128:(c + 1) * 128], otok[:, t, c * 128:(c + 1) * 128], ident[:, :])
                                            for c in range(4)], reads=[b_otok, b_c], writes=[tbb])
            op("act", lambda e, tv=tv, t=t: e.copy(onsaT[:, :, t * 128:(t + 1) * 128],
                                                   tv[:, 0:512].rearrange("p (c j) -> p c j", c=4)),
               reads=[tbb], writes=[b_onsaT])

    def p2_gdn(self):
        nc, fw, op, A = self.nc, self.fw, self.fw.op, self.alloc
        ident, identf, b_c = self.ident, self.identf, self.b_c
        GT, b_GT, ygT, b_ygT = self.GT, self.b_GT, self.ygT, self.b_ygT
        cd, w_in3 = self.cd, self.w_in3
        din = self.din
        self.banks_rot = self.banks
        QKV = A([12, S], BF16); b_QKV = [self.buf("QKV%d" % j) for j in range(12)]
        GS = A([NT, 512], BF16); b_GS = self.buf("GS")
        gmat = A([8, 128], F32); b_gm = self.buf("gmat")
        onesb = A([128], BF16)
        ident4 = A([4, 128], BF16)
        normw = A([512], F32)
        convw = A([12, 4], F32)
        dtb = A([64], F32); alog = A([64], F32)
        sc = {n: A([64], F32) for n in ("g", "gcum", "gtot", "eg", "egt", "kds", "bge", "nb", "tmp")}
        b_sc = self.buf("gscal")
        op("sp", lambda e: e.dma_start(out=gmat, in_=cd["gmat"].rearrange("m p j -> p m j")), writes=[b_gm], dma=1, key="g0")
        op("sp", lambda e: e.dma_start(out=normw, in_=din["gdn_normw"]), writes=[b_gm], dma=1, key="g1")
        op("sp", lambda e: e.dma_start(out=convw, in_=din["gdn_convw"]), writes=[b_gm], dma=1, key="g2")
        op("sp", lambda e: e.dma_start(out=dtb, in_=din["gdn_dtb"]), writes=[b_gm], dma=1, key="g3")
        op("sp", lambda e: e.dma_start(out=alog, in_=din["gdn_alog"]), writes=[b_gm], dma=1, key="g4")
        op("dve", lambda e: e.memset(onesb, 1.0), writes=[b_gm])
        op("dve", lambda e: [e.tensor_copy(ident4[:, h, :], ident) for h in range(4)], reads=[b_c], writes=[b_gm])
        Tri, SU, Mli, Mls, MTi, onesf, Mbd, M21 = (gmat[:, i, :] for i in range(8))

        GTf = GT.rearrange("p t c -> p (t c)")
        g_, gcum, gtot, eg, egt, kds, bge, nb, tmp = (sc[n] for n in ("g", "gcum", "gtot", "eg", "egt", "kds", "bge", "nb", "tmp"))
        v3 = lambda a: a.rearrange("p (t h) -> p t h", h=4)
        op("dve", lambda e: e.tensor_tensor(v3(tmp), GT[:, :, 28:32], v3(dtb), op=ALU.add), reads=[b_GT, b_gm], writes=[b_sc])
        a1, a2, a3, a4, a5 = gcum, gtot, eg, egt, kds
        S_ = lambda fn: op("dve", fn, reads=[b_sc], writes=[b_sc])
        S_(lambda e: e.tensor_scalar_mul(a1, tmp, -1.0))
        S_(lambda e: e.tensor_tensor(a1, a1, tmp, op=ALU.max))
        op("act", lambda e: e.activation(a1, a1, AF.Exp, scale=-1.0), reads=[b_sc], writes=[b_sc])
        S_(lambda e: e.tensor_scalar_add(a2, a1, 2.0))
        S_(lambda e: e.reciprocal(a2, a2))
        S_(lambda e: e.tensor_tensor(a2, a2, a1, op=ALU.mult))
        S_(lambda e: e.tensor_tensor(a3, a2, a2, op=ALU.mult))
        S_(lambda e: e.tensor_scalar(a4, a3, 1.0 / 9.0, 1.0 / 7.0, op0=ALU.mult, op1=ALU.add))
        for cst in (1.0 / 5.0, 1.0 / 3.0, 1.0):
            S_(lambda e: e.tensor_tensor(a4, a4, a3, op=ALU.mult))
            S_(lambda e, cst=cst: e.tensor_scalar_add(a4, a4, cst))
        S_(lambda e: e.tensor_tensor(a4, a4, a2, op=ALU.mult))
        S_(lambda e: e.tensor_scalar_max(a5, tmp, 0.0))
        S_(lambda e: e.scalar_tensor_tensor(tmp, a4, 2.0, a5, op0=ALU.mult, op1=ALU.add))
        op("act", lambda e: e.activation(g_, alog, AF.Exp), reads=[b_gm], writes=[b_sc])
        op("dve", lambda e: e.scalar_tensor_tensor(g_, tmp, -1.0, g_, op0=ALU.mult, op1=ALU.mult), reads=[b_sc], writes=[b_sc])
        bkA, bbA = self.bank()
        op("pe", lambda e: [e.matmul(bkA[:, 0:64], lhsT=Tri, rhs=g_, start=True, stop=True),
                            e.matmul(bkA[:, 64:128], lhsT=onesf, rhs=g_, start=True, stop=True)],
           reads=[b_sc, b_gm], writes=[bbA])
        op("dve", lambda e: e.tensor_copy(gcum, bkA[:, 0:64]), reads=[bbA], writes=[b_sc])
        op("dve", lambda e: e.tensor_copy(gtot, bkA[:, 64:128]), reads=[bbA], writes=[b_sc])
        op("act", lambda e: e.activation(eg, gcum, AF.Exp), reads=[b_sc], writes=[b_sc])
        op("act", lambda e: e.activation(egt, gtot, AF.Exp), reads=[b_sc], writes=[b_sc])
        op("dve", lambda e: e.tensor_tensor(kds, gtot, gcum, op=ALU.subtract), reads=[b_sc], writes=[b_sc])
        op("act", lambda e: e.activation(kds, kds, AF.Exp), reads=[b_sc], writes=[b_sc])
        op("dve", lambda e: e.tensor_tensor(v3(bge), GT[:, :, 24:28], v3(eg), op=ALU.mult), reads=[b_sc, b_GT], writes=[b_sc])
        op("dve", lambda e: e.tensor_scalar_mul(v3(nb), GT[:, :, 24:28], -1.0), reads=[b_GT], writes=[b_sc])

        mark2 = self.off
        self.load_xT()
        xT, b_xT = self.xT, self.b_xT
        raw = [A([S + 3], F32) for _ in range(2)]; b_raw = [self.buf("raw%d" % i) for i in range(2)]
        cacc2 = [A([S], F32) for _ in range(2)]; b_cacc2 = [self.buf("cacc%d" % i) for i in range(2)]
        sq = A([S], BF16); b_sq = self.buf("sq")
        rs = [A([512], F32) for _ in range(2)]; b_rs = [self.buf("rs%d" % i) for i in range(2)]
        for i in range(2):
            op("dve", lambda e, i=i: e.memset(raw[i][:, 0:3], 0.0), writes=[b_raw[i]])
        for j in range(12):
            wj, bwj = self.ldw(w_in3[:, :, 1304 + j * 128:1304 + (j + 1) * 128], 128)
            rw, brw = raw[j % 2], b_raw[j % 2]
            cacc, b_cacc = cacc2[j % 2], b_cacc2[j % 2]
            self.proj_fm(wj, bwj, 0, 128, None, None,
                         evac=lambda tt, bk, bb, rw=rw, brw=brw: op(
                             "act", lambda e: e.copy(rw[:, 3 + tt * 512:3 + (tt + 1) * 512], bk[:, 0:512]), reads=[bb], writes=[brw]))
            op("act", lambda e, rw=rw, j=j, cacc=cacc: e.mul(cacc, rw[:, 3:3 + S], convw[:, j, 3:4]),
               reads=[brw, b_gm], writes=[b_cacc])
            for tap in (2, 1, 0):
                op("dve", lambda e, rw=rw, j=j, tap=tap, cacc=cacc: e.scalar_tensor_tensor(
                    cacc, rw[:, tap:tap + S], convw[:, j, tap:tap + 1], cacc, op0=ALU.mult, op1=ALU.add),
                   reads=[brw, b_gm, b_cacc], writes=[b_cacc])
            if j >= 8:
                op("act", lambda e, j=j, cacc=cacc: e.activation(QKV[:, j, :], cacc, AF.Silu), reads=[b_cacc], writes=[b_QKV[j]])
                continue
            op("act", lambda e, cacc=cacc: e.activation(cacc, cacc, AF.Silu), reads=[b_cacc], writes=[b_cacc])
            op("act", lambda e, cacc=cacc: e.activation(sq, cacc, AF.Square), reads=[b_cacc], writes=[b_sq])
            scale = (128.0 ** -0.5) if j < 4 else 1.0
            for tt in range(4):
                bk, bb = self.bank()
                r_, br_ = rs[tt % 2], b_rs[tt % 2]
                op("pe", lambda e, bk=bk, tt=tt: e.matmul(bk[:, 0:512], lhsT=onesb, rhs=sq[:, tt * 512:(tt + 1) * 512],
                                                          start=True, stop=True), reads=[b_sq, b_gm], writes=[bb])
                op("act", lambda e, bk=bk, r_=r_: e.activation(r_, bk[:, 0:512], AF.Sqrt, bias=1e-6), reads=[bb], writes=[br_])
                op("dve", lambda e, r_=r_: e.reciprocal(r_, r_), reads=[br_], writes=[br_])
                op("dve", lambda e, r_=r_, j=j, tt=tt, scale=scale, cacc=cacc: e.scalar_tensor_tensor(
                    QKV[:, j, tt * 512:(tt + 1) * 512], cacc[:, tt * 512:(tt + 1) * 512], scale, r_, op0=ALU.mult, op1=ALU.mult),
                   reads=[br_, b_cacc], writes=[b_QKV[j]])
        wg, bwg = self.ldw(w_in3[:, :, 2848:3360], 512)
        for t in range(NT):
            bk, bb = self.bank()
            op("pe", lambda e, bk=bk, t=t: [
                e.matmul(bk[:, 0:512], lhsT=xT[:, k, t * 128:(t + 1) * 128], rhs=wg[:, k, 0:512],
                         start=(k == 0), stop=(k == 7)) for k in range(8)], reads=[bwg] + b_xT, writes=[bb])
            op("act", lambda e, bk=bk, t=t: e.activation(GS[:, t, :], bk[:, 0:512], AF.Silu), reads=[bb], writes=[b_GS])
        self.barrier()
        self.off = mark2

        def set_bufs():
            d = {}
            for n in ("Pa", "Pb", "PTa", "PTb", "TTa", "TTb", "aqkT", "kbg", "kd", "vb", "negwT", "X21", "Tbd", "W21"):
                d[n] = A([4, 128], BF16)
            d["buf"] = self.buf("gset%d" % len(self.bufs))
            return d

        def early_bufs():
            d = {}
            for n in ("gsu", "ed", "edT", "dstr", "d21"):
                d[n] = A([4, 128], F32)
            d["buf"] = self.buf("gearly%d" % len(self.bufs))
            return d
        sets = [set_bufs() for _ in range(4)]
        earlys = [early_bufs() for _ in range(2)]
        Sf = A([4, 128], F32); Sb = A([4, 128], BF16); b_S = self.buf("gS")
        vnew = A([4, 128], BF16); avsb = A([4, 128], F32); of = A([4, 128], F32); osq = A([4, 128], F32)
        ytok = A([512], BF16); ssq = A([8], F32)
        b_w = self.buf("gwork")
        op("dve", lambda e: e.memset(Sf, 0.0), writes=[b_S])
        op("dve", lambda e: e.memset(Sb, 0.0), writes=[b_S])
        f2 = lambda a: a.rearrange("p h d -> p (h d)")

        def setup_steps(t, d, ea):
            bs = d["buf"]
            eb = ea["buf"]
            tok = slice(t * 128, (t + 1) * 128)
            qT = lambda h: QKV[:, h, tok]
            kT = lambda h: QKV[:, 4 + h, tok]
            vT = lambda h: QKV[:, 8 + h, tok]
            col = lambda a, h: a[:, t * 4 + h:t * 4 + h + 1]
            st = {}
            steps = []

            def s1():
                op("dve", lambda e: [e.tensor_scalar_mul(ea["gsu"][:, h, :], SU, col(g_, h)) for h in range(4)],
                   reads=[b_sc, b_gm], writes=[eb])
                bD, bbD = self.bank(); bDT, bbDT = self.bank()
                op("pe", lambda e: [e.matmul(bD[:, h * 128:(h + 1) * 128], lhsT=Tri, rhs=ea["gsu"][:, h, :], start=True, stop=True)
                                    for h in range(4)], reads=[eb, b_gm], writes=[bbD])
                op("pe", lambda e: [e.matmul(bDT[:, h * 128:(h + 1) * 128], lhsT=ea["gsu"][:, h, :], rhs=Tri, start=True, stop=True)
                                    for h in range(4)], reads=[eb, b_gm], writes=[bbDT])
                op("act", lambda e: e.activation(f2(ea["ed"]), bD[:, 0:512], AF.Exp), reads=[bbD], writes=[eb])
                op("act", lambda e: e.activation(f2(ea["edT"]), bDT[:, 0:512], AF.Exp), reads=[bbDT], writes=[eb])
                op("dve", lambda e: [e.tensor_tensor(ea["dstr"][:, h, :], ea["ed"][:, h, :], Mbd, op=ALU.mult) for h in range(4)],
                   reads=[eb, b_gm], writes=[eb])
                op("dve", lambda e: [e.tensor_tensor(ea["d21"][:, h, :], ea["ed"][:, h, :], M21, op=ALU.mult) for h in range(4)],
                   reads=[eb, b_gm], writes=[eb])
                op("dve", lambda e: [e.tensor_tensor(ea["edT"][:, h, :], ea["edT"][:, h, :], MTi, op=ALU.mult) for h in range(4)],
                   reads=[eb, b_gm], writes=[eb])
            steps.append(s1)

            def s2():
                bG, bbG = self.bank(); bA, bbA_ = self.bank()
                rq = b_QKV[0:8]
                op("pe", lambda e: [e.matmul(bG[:, h * 128:(h + 1) * 128], lhsT=kT(h), rhs=kT(h), start=True, stop=True)
                                    for h in range(4)], reads=rq, writes=[bbG])
                op("pe", lambda e: [e.matmul(bA[:, h * 128:(h + 1) * 128], lhsT=kT(h), rhs=qT(h), start=True, stop=True)
                                    for h in range(4)], reads=rq, writes=[bbA_])
                op("dve", lambda e: [e.scalar_tensor_tensor(d["Pa"][:, h, :], bG[:, h * 128:(h + 1) * 128], col(nb, h),
                                                            ea["dstr"][:, h, :], op0=ALU.mult, op1=ALU.mult) for h in range(4)],
                   reads=[bbG, eb, b_sc], writes=[bs])
                op("dve", lambda e: [e.scalar_tensor_tensor(d["X21"][:, h, :], bG[:, h * 128:(h + 1) * 128], col(nb, h),
                                                            ea["d21"][:, h, :], op0=ALU.mult, op1=ALU.mult) for h in range(4)],
                   reads=[bbG, eb, b_sc], writes=[bs])
                op("dve", lambda e: e.tensor_tensor(f2(d["aqkT"]), bA[:, 0:512], f2(ea["edT"]), op=ALU.mult),
                   reads=[bbA_, eb], writes=[bs])
                bT, bbT = self.bank()
                tv = bT.bitcast(BF16)
                op("pe", lambda e: [e.transpose(tv[:, h * 128:(h + 1) * 128], d["Pa"][:, h, :], ident) for h in range(4)],
                   reads=[bs, b_c], writes=[bbT])
                op("act", lambda e: e.copy(f2(d["PTa"]), tv[:, 0:512]), reads=[bbT], writes=[bs])
                op("dve", lambda e: e.tensor_tensor(f2(d["TTa"]), tv[:, 0:512], f2(ident4), op=ALU.add),
                   reads=[bbT, b_gm], writes=[bs])
                st["P"], st["PT"], st["TT"] = "Pa", "PTa", "TTa"
            steps.append(s2)

            def lvl(i):
                def f():
                    P, PT_, TT = d[st["P"]], d[st["PT"]], d[st["TT"]]
                    nP = "Pb" if st["P"] == "Pa" else "Pa"
                    nPT = "PTb" if st["PT"] == "PTa" else "PTa"
                    nTT = "TTb" if st["TT"] == "TTa" else "TTa"
                    bP, bbP = self.bank()
                    op("pe", lambda e: [e.matmul(bP[:, h * 128:(h + 1) * 128], lhsT=PT_[:, h, :], rhs=P[:, h, :], start=True, stop=True)
                                        for h in range(4)], reads=[bs], writes=[bbP])
                    if i < 5:
                        bPT, bbPT = self.bank()
                        op("pe", lambda e: [e.matmul(bPT[:, h * 128:(h + 1) * 128], lhsT=P[:, h, :], rhs=PT_[:, h, :],
                                                     start=True, stop=True) for h in range(4)], reads=[bs], writes=[bbPT])
                    op("act", lambda e: e.copy(f2(d[nP]), bP[:, 0:512]), reads=[bbP], writes=[bs])
                    if i < 5:
                        op("dve", lambda e: e.tensor_copy(f2(d[nPT]), bPT[:, 0:512]), reads=[bbPT], writes=[bs])
                    bTT, bbTT = self.bank()
                    op("pe", lambda e: [e.matmul(bTT[:, h * 128:(h + 1) * 128], lhsT=d[nP][:, h, :], rhs=TT[:, h, :],
                                                 start=True, stop=True) for h in range(4)], reads=[bs], writes=[bbTT])
                    op("dve", lambda e: e.tensor_tensor(f2(d[nTT]), bTT[:, 0:512], f2(TT), op=ALU.add), reads=[bbTT, bs], writes=[bs])
                    st["P"], st["PT"], st["TT"] = nP, nPT, nTT
                return f
            for i in range(1, 6):
                steps.append(lvl(i))

            def s8():
                TT = d[st["TT"]]
                nTT = "TTb" if st["TT"] == "TTa" else "TTa"
                bT2, bbT2 = self.bank()
                t2 = bT2.bitcast(BF16)
                op("pe", lambda e: [e.transpose(t2[:, h * 128:(h + 1) * 128], TT[:, h, :], ident) for h in range(4)],
                   reads=[bs, b_c], writes=[bbT2])
                op("act", lambda e: e.copy(f2(d["Tbd"]), t2[:, 0:512]), reads=[bbT2], writes=[bs])
                bW2, bbW2 = self.bank()
                op("pe", lambda e: [e.matmul(bW2[:, h * 128:(h + 1) * 128], lhsT=d["X21"][:, h, :], rhs=TT[:, h, :], start=True, stop=True)
                                    for h in range(4)], reads=[bs], writes=[bbW2])
                op("dve", lambda e: e.tensor_copy(f2(d["W21"]), bW2[:, 0:512]), reads=[bbW2], writes=[bs])
                bZ, bbZ = self.bank()
                op("pe", lambda e: [e.matmul(bZ[:, h * 128:(h + 1) * 128], lhsT=d["Tbd"][:, h, :], rhs=d["W21"][:, h, :], start=True, stop=True)
                                    for h in range(4)], reads=[bs], writes=[bbZ])
                op("dve", lambda e: e.tensor_tensor(f2(d[nTT]), bZ[:, 0:512], f2(TT), op=ALU.add), reads=[bbZ, bs], writes=[bs])
                st["TT"] = nTT
            steps.append(s8)

            def s9():
                bK, bbK = self.bank(); bV, bbV = self.bank()
                tk = bK.bitcast(BF16); tvv = bV.bitcast(BF16)
                op("pe", lambda e: [e.transpose(tk[:, h * 128:(h + 1) * 128], kT(h), ident) for h in range(4)],
                   reads=b_QKV[4:8] + [b_c], writes=[bbK])
                op("pe", lambda e: [e.transpose(tvv[:, h * 128:(h + 1) * 128], vT(h), ident) for h in range(4)],
                   reads=b_QKV[8:12] + [b_c], writes=[bbV])
                op("act", lambda e: [e.mul(d["kbg"][:, h, :], tk[:, h * 128:(h + 1) * 128], col(bge, h)) for h in range(4)],
                   reads=[bbK, b_sc], writes=[bs])
                op("dve", lambda e: [e.tensor_scalar_mul(d["kd"][:, h, :], tk[:, h * 128:(h + 1) * 128], col(kds, h)) for h in range(4)],
                   reads=[bbK, b_sc], writes=[bs])
                op("dve", lambda e: [e.tensor_scalar_mul(d["vb"][:, h, :], tvv[:, h * 128:(h + 1) * 128], GT[:, t, 24 + h:25 + h])
                                     for h in range(4)], reads=[bbV, b_GT], writes=[bs])
                TT = d[st["TT"]]
                bW, bbW = self.bank()
                op("pe", lambda e: [e.matmul(bW[:, h * 128:(h + 1) * 128], lhsT=d["kbg"][:, h, :], rhs=TT[:, h, :], start=True, stop=True)
                                    for h in range(4)], reads=[bs], writes=[bbW])
                op("act", lambda e: e.mul(f2(d["negwT"]), bW[:, 0:512], -1.0), reads=[bbW], writes=[bs])
                st["TTfin"] = TT
            steps.append(s9)
            return steps, st

        def scan_steps(t, d, st):
            bs = d["buf"]
            tok = slice(t * 128, (t + 1) * 128)
            qT = lambda h: QKV[:, h, tok]
            col = lambda a, h: a[:, t * 4 + h:t * 4 + h + 1]
            b_o = self.b_gout
            sh = {}

            def p1():
                TT = st["TTfin"]
                bV, bbV = self.bank()
                op("pe", lambda e: [x for h in range(4) for x in (
                    e.matmul(bV[:, h * 128:(h + 1) * 128], lhsT=TT[:, h, :], rhs=d["vb"][:, h, :], start=True, stop=False),
                    e.matmul(bV[:, h * 128:(h + 1) * 128], lhsT=d["negwT"][:, h, :], rhs=Sb[:, h, :], start=False, stop=True))],
                   reads=[bs, b_S], writes=[bbV])
                op("act", lambda e: e.copy(f2(vnew), bV[:, 0:512]), reads=[bbV], writes=[b_w])

            def p2():
                (bQ, bbQ), (bAV, bbAV), (bS_, bbS) = self.banks[0], self.banks[1], self.banks[2]
                sh.update(bQ=bQ, bbQ=bbQ, bAV=bAV, bbAV=bbAV, bS_=bS_, bbS=bbS)
                op("pe", lambda e: [e.matmul(bQ[:, h * 128:(h + 1) * 128], lhsT=qT(h), rhs=Sb[:, h, :], start=True, stop=True)
                                    for h in range(4)], reads=b_QKV[0:4] + [b_S], writes=[bbQ])
                op("pe", lambda e: [e.matmul(bS_[:, h * 128:(h + 1) * 128], lhsT=d["kd"][:, h, :], rhs=vnew[:, h, :], start=True, stop=True)
                                    for h in range(4)], reads=[bs, b_w], writes=[bbS])
                op("pe", lambda e: [e.matmul(bAV[:, h * 128:(h + 1) * 128], lhsT=d["aqkT"][:, h, :], rhs=vnew[:, h, :], start=True, stop=True)
                                    for h in range(4)], reads=[bs, b_w], writes=[bbAV])

            def p3():
                bS_, bbS = sh["bS_"], sh["bbS"]
                op("dve", lambda e: [e.scalar_tensor_tensor(Sf[:, h, :], Sf[:, h, :], col(egt, h), bS_[:, h * 128:(h + 1) * 128],
                                                            op0=ALU.mult, op1=ALU.add) for h in range(4)],
                   reads=[bbS, b_S, b_sc], writes=[b_S])
                op("act", lambda e: e.copy(f2(Sb), f2(Sf)), reads=[b_S], writes=[b_S])

            def p4():
                bQ, bbQ, bAV, bbAV = sh["bQ"], sh["bbQ"], sh["bAV"], sh["bbAV"]
                op("act", lambda e: e.copy(f2(avsb), bAV[:, 0:512]), reads=[bbAV], writes=[b_o])
                op("dve", lambda e: [e.scalar_tensor_tensor(of[:, h, :], bQ[:, h * 128:(h + 1) * 128], col(eg, h), avsb[:, h, :],
                                                            op0=ALU.mult, op1=ALU.add) for h in range(4)],
                   reads=[bbQ, b_o, b_sc], writes=[b_o])

            def p5():
                op("act", lambda e: e.activation(f2(osq), f2(of), AF.Square), reads=[b_o], writes=[b_o])
                op("dve", lambda e: e.tensor_reduce(ssq[:, 0:4], osq, axis=AX.X, op=ALU.add), reads=[b_o], writes=[b_o])
                op("act", lambda e: e.activation(ssq[:, 0:4], ssq[:, 0:4], AF.Sqrt, bias=1e-6, scale=1.0 / 128.0), reads=[b_o], writes=[b_o])
                op("dve", lambda e: e.reciprocal(ssq[:, 4:8], ssq[:, 0:4]), reads=[b_o], writes=[b_o])
                op("dve", lambda e: [e.scalar_tensor_tensor(of[:, h, :], of[:, h, :], ssq[:, 4 + h:5 + h], normw[:, h * 128:(h + 1) * 128],
                                                            op0=ALU.mult, op1=ALU.mult) for h in range(4)],
                   reads=[b_o, b_gm], writes=[b_o])
                op("dve", lambda e: e.tensor_tensor(ytok, f2(of), GS[:, t, :], op=ALU.mult), reads=[b_o, b_GS], writes=[b_o])

            def p6():
                if self.dbg == "gdn":
                    op("dve", lambda e: e.tensor_copy(dbg_stg[0], ytok), reads=[b_o], writes=[dbg_stg[1]])
                    op("sp", lambda e: e.dma_start(out=self.dbg_d[t * 128:(t + 1) * 128, :], in_=dbg_stg[0]),
                       reads=[dbg_stg[1]], dma=1, key="dbgo")
                    return
                bY, bbY = self.bank()
                ty = bY.bitcast(BF16)
                op("pe", lambda e: [e.transpose(ty[:, c * 128:(c + 1) * 128], ytok[:, c * 128:(c + 1) * 128], ident) for c in range(4)],
                   reads=[b_o, b_c], writes=[bbY])
                op("act", lambda e: e.copy(ygT[:, :, tok], ty[:, 0:512].rearrange("p (c j) -> p c j", c=4)), reads=[bbY], writes=[b_ygT])
            return [p1, p2, p3, p4, p5, p6]

        self.b_gout = self.buf("gout")
        dbg_stg = None
        if self.dbg == "gdn":
            dbg_stg = (A([512], F32), self.buf("dstg"))

        self.banks_rot = self.banks[3:8]

        def zip_emit(streams):
            for k in range(max(len(x) for x in streams)):
                for x in streams:
                    if k < len(x):
                        x[k]()
        info = {}

        def mk_setup(t):
            steps, st = setup_steps(t, sets[t % 4], earlys[t % 2])
            info[t] = st
            return steps
        zip_emit([mk_setup(0), mk_setup(1)])
        for n in range(NT // 2):
            streams = []
            if n + 1 < NT // 2:
                streams += [mk_setup(2 * n + 2), mk_setup(2 * n + 3)]
            sc_ = []
            for t in (2 * n, 2 * n + 1):
                sc_ += scan_steps(t, sets[t % 4], info[t])
            streams.append(sc_)
            zip_emit(streams)

    def ln_tile(self, pre, gT, bT, out, b_pre, b_ln, tmp, st, b_out):
        op = self.fw.op
        op("dve", lambda e: e.tensor_reduce(st[:, 0:1], pre, axis=AX.X, op=ALU.add), reads=[b_pre], writes=[b_ln])
        op("dve", lambda e: e.tensor_scalar_mul(st[:, 1:2], st[:, 0:1], -1.0 / D), reads=[b_ln], writes=[b_ln])
        op("act", lambda e: e.activation(tmp, pre, AF.Square, bias=st[:, 1:2], accum_out=st[:, 2:3]),
           reads=[b_pre, b_ln], writes=[b_ln])
        op("act", lambda e: e.activation(st[:, 3:4], st[:, 2:3], AF.Sqrt, bias=1e-5, scale=1.0 / D), reads=[b_ln], writes=[b_ln])
        op("dve", lambda e: e.reciprocal(st[:, 4:5], st[:, 3:4]), reads=[b_ln], writes=[b_ln])
        op("dve", lambda e: e.tensor_tensor(st[:, 5:6], st[:, 1:2], st[:, 4:5], op=ALU.mult), reads=[b_ln], writes=[b_ln])
        op("act", lambda e: e.activation(tmp, pre, AF.Identity, bias=st[:, 5:6], scale=st[:, 4:5]), reads=[b_pre, b_ln], writes=[b_ln])
        op("dve", lambda e: e.tensor_tensor(tmp, tmp, gT, op=ALU.mult), reads=[b_ln, self.b_lnc], writes=[b_ln])
        op("dve", lambda e: e.tensor_tensor(out, tmp, bT, op=ALU.add), reads=[b_ln, self.b_lnc], writes=[b_out])

    def p34(self):
        nc, fw, op, A = self.nc, self.fw, self.fw.op, self.alloc
        ident, b_c = self.ident, self.b_c
        onsaT, b_onsaT, ygT, b_ygT = self.onsaT, self.b_onsaT, self.ygT, self.b_ygT
        x1T = self.oy
        din, w_in3 = self.din, self.w_in3
        self.banks_rot = self.banks
        x_d, out_d = din["x"], self.out_d
        wno_d = din["w_nsa_out"].rearrange("(k p) n -> p k n", p=128)
        wgo_d = din["w_gdn_out"].rearrange("(k p) n -> p k n", p=128)
        wo_d = din["w_o"].rearrange("(k p) n -> p k n", p=128)
        wup_d = din["ffn_w_up"].rearrange("(k p) n -> p k n", p=128)
        wdn_d = din["ffn_w_down"].rearrange("(j p) n -> p j n", p=128)
        x1 = A([NT, D], F32); b_x1 = [self.buf("x1_%d" % i) for i in range(NT)]
        lnc = A([2, D], F32); self.b_lnc = self.buf("lnc")
        lnt2 = [A([D], F32) for _ in range(2)]; lst2 = [A([8], F32) for _ in range(2)]
        b_ln2 = [self.buf("lnt%d" % i) for i in range(2)]
        self.wb = [(A([8, 512], BF16), self.buf("wb%d" % i)) for i in range(4)]
        self.wrot = 0
        mark3 = self.off
        for i, n in enumerate(("ln1_g", "ln1_b")):
            op("sp", lambda e, i=i, n=n: e.dma_start(out=lnc[:, i, :], in_=din[n]), writes=[self.b_lnc], dma=1, key="lnc%d" % i)
        wno = A([4, D], BF16); wgo = A([4, D], BF16); b_wout = self.buf("wout")
        op("pool", lambda e: e.dma_start(out=wno, in_=wno_d), writes=[b_wout], dma=1, key="wout0")
        op("pool", lambda e: e.dma_start(out=wgo, in_=wgo_d), writes=[b_wout], dma=1, key="wout1")
        xts = A([8, 512], BF16); b_xts = self.buf("xts")
        mixT = A([8, 512], BF16); b_mix = self.buf("mixT")
        sg = [A([512], F32) for _ in range(8)]; b_sg = [self.buf("sg%d" % i) for i in range(8)]
        xin = [A([D], F32) for _ in range(2)]; b_xin = [self.buf("xin%d" % i) for i in range(2)]
        x1b = A([D], BF16); b_x1b = self.buf("x1b")
        xn = [0]

        def wslot():
            i = self.wrot % 4
            self.wrot += 1
            return self.wb[i]

        for tt in range(4):
            tok = slice(tt * 512, (tt + 1) * 512)
            op("pool", lambda e, tt=tt: [e.dma_start(out=xts[:, k, :], in_=din["xT"][k * 128:(k + 1) * 128, tt * 512:(tt + 1) * 512])
                                         for k in range(8)], writes=[b_xts], dma=8)
            for cq in range(2):
                gwa, bgwa = wslot()
                gwb, bgwb = wslot()
                op("pool", lambda e, gwa=gwa, cq=cq: [e.dma_start(out=gwa[:, 0:4, :], in_=w_in3[:, 0:4, 3360 + cq * 512:3360 + (cq + 1) * 512]),
                                                      e.dma_start(out=gwa[:, 4:8, :], in_=w_in3[:, 4:8, 3360 + cq * 512:3360 + (cq + 1) * 512])],
                   writes=[bgwa], dma=2)
                op("pool", lambda e, gwb=gwb, cq=cq: [e.dma_start(out=gwb[:, 0:4, :], in_=w_in3[:, 0:4, 4384 + cq * 512:4384 + (cq + 1) * 512]),
                                                      e.dma_start(out=gwb[:, 4:8, :], in_=w_in3[:, 4:8, 4384 + cq * 512:4384 + (cq + 1) * 512])],
                   writes=[bgwb], dma=2)
                for c4 in range(4):
                    c = cq * 4 + c4
                    bga, bbga = self.bank(); bgb, bbgb = self.bank(); bya, bbya = self.bank(); byb, bbyb = self.bank()
                    op("pe", lambda e, gwa=gwa, bga=bga, c4=c4: [e.matmul(bga[:, 0:512], lhsT=gwa[:, k, c4 * 128:(c4 + 1) * 128], rhs=xts[:, k, :],
                                                                          start=(k == 0), stop=(k == 7)) for k in range(8)],
                       reads=[bgwa, b_xts], writes=[bbga])
                    op("pe", lambda e, gwb=gwb, bgb=bgb, c4=c4: [e.matmul(bgb[:, 0:512], lhsT=gwb[:, k, c4 * 128:(c4 + 1) * 128], rhs=xts[:, k, :],
                                                                          start=(k == 0), stop=(k == 7)) for k in range(8)],
                       reads=[bgwb, b_xts], writes=[bbgb])
                    op("pe", lambda e, c=c, bya=bya, tok=tok: [e.matmul(bya[:, 0:512], lhsT=wno[:, k, c * 128:(c + 1) * 128], rhs=onsaT[:, k, tok],
                                                                         start=(k == 0), stop=(k == 3)) for k in range(4)],
                       reads=[b_wout, b_onsaT], writes=[bbya])
                    op("pe", lambda e, c=c, byb=byb, tok=tok: [e.matmul(byb[:, 0:512], lhsT=wgo[:, k, c * 128:(c + 1) * 128], rhs=ygT[:, k, tok],
                                                                         start=(k == 0), stop=(k == 3)) for k in range(4)],
                       reads=[b_wout, b_ygT], writes=[bbyb])
                    o_ = (c % 2) * 4
                    op("act", lambda e, bga=bga, o_=o_: e.activation(sg[o_], bga[:, 0:512], AF.Sigmoid), reads=[bbga], writes=[b_sg[o_]])
                    op("act", lambda e, bgb=bgb, o_=o_: e.activation(sg[o_ + 1], bgb[:, 0:512], AF.Sigmoid), reads=[bbgb], writes=[b_sg[o_ + 1]])
                    op("dve", lambda e, bya=bya, o_=o_: e.tensor_tensor(sg[o_ + 2], bya[:, 0:512], sg[o_], op=ALU.mult),
                       reads=[bbya, b_sg[o_]], writes=[b_sg[o_ + 2]])
                    op("dve", lambda e, byb=byb, o_=o_: e.tensor_tensor(sg[o_ + 3], byb[:, 0:512], sg[o_ + 1], op=ALU.mult),
                       reads=[bbyb, b_sg[o_ + 1]], writes=[b_sg[o_ + 3]])
                    op("dve", lambda e, c=c, o_=o_: e.tensor_tensor(mixT[:, c, :], sg[o_ + 2], sg[o_ + 3], op=ALU.add),
                       reads=[b_sg[o_ + 2], b_sg[o_ + 3]], writes=[b_mix])
            wos = []
            for hf in range(2):
                wo_, bwo_ = wslot()
                op("pool", lambda e, wo_=wo_, hf=hf: [e.dma_start(out=wo_[:, 0:4, :], in_=wo_d[:, 0:4, hf * 512:(hf + 1) * 512]),
                                                      e.dma_start(out=wo_[:, 4:8, :], in_=wo_d[:, 4:8, hf * 512:(hf + 1) * 512])],
                   writes=[bwo_], dma=2)
                wos.append((wo_, bwo_))
            for s4 in range(4):
                t = tt * 4 + s4
                xi, bxi = xin[xn[0] % 2], b_xin[xn[0] % 2]
                xn[0] += 1
                op("sp", lambda e, xi=xi, t=t: e.dma_start(out=xi, in_=x_d[t * 128:(t + 1) * 128, :]), writes=[bxi], dma=1)
                for hf in range(2):
                    bk, bb = self.bank()
                    wo_, bwo_ = wos[hf]
                    op("pe", lambda e, bk=bk, wo_=wo_, s4=s4: [e.matmul(bk[:, 0:512], lhsT=mixT[:, k, s4 * 128:(s4 + 1) * 128], rhs=wo_[:, k, 0:512],
                                                                         start=(k == 0), stop=(k == 7)) for k in range(8)],
                       reads=[bwo_, b_mix], writes=[bb])
                    op("dve", lambda e, bk=bk, xi=xi, hf=hf: e.scalar_tensor_tensor(xi[:, hf * 512:(hf + 1) * 512], xi[:, hf * 512:(hf + 1) * 512],
                                                                                   ALPHA, bk[:, 0:512], op0=ALU.mult, op1=ALU.add),
                       reads=[bb, bxi], writes=[bxi])
                self.ln_tile(xi, lnc[:, 0, :], lnc[:, 1, :], x1[:, t, :], bxi, b_ln2[t % 2], lnt2[t % 2], lst2[t % 2], b_x1[t])
                op("act", lambda e, t=t: e.copy(x1b, x1[:, t, :]), reads=[b_x1[t]], writes=[b_x1b])
                bk, bb = self.bank()
                tv = bk.bitcast(BF16)
                op("pe", lambda e, tv=tv: [e.transpose(tv[:, k * 128:(k + 1) * 128], x1b[:, k * 128:(k + 1) * 128], ident) for k in range(8)],
                   reads=[b_x1b, b_c], writes=[bb])
                op("act", lambda e, tv=tv, t=t: e.copy(x1T[:, :, t * 128:(t + 1) * 128], tv[:, 0:1024].rearrange("p (k j) -> p k j", k=8)),
                   reads=[bb], writes=[b_onsaT, b_ygT])
        if self.dbg == "x1":
            for t in range(NT):
                op("sp", lambda e, t=t: e.dma_start(out=self.dbg_d[t * 128:(t + 1) * 128, :], in_=x1[:, t, 0:512]),
                   reads=[b_x1[t]], dma=1, key="dbgo")
            return
        self.barrier()
        self.off = mark3
        b_x1T = self.buf("x1T")
        for i, n in enumerate(("ln2_g", "ln2_b")):
            op("sp", lambda e, i=i, n=n: e.dma_start(out=lnc[:, i, :], in_=din[n]), writes=[self.b_lnc], dma=1, key="lnc%d" % i)
        fcw = A([44, 3], F32)
        op("sp", lambda e: e.dma_start(out=fcw, in_=din["ffn_convw"]), writes=[self.b_lnc], dma=1, key="lnc4")
        wdg = [A([4, D], BF16) for _ in range(2)]; b_wdg = [self.buf("wdg%d" % i) for i in range(2)]
        aTg = [A([4, 512], BF16) for _ in range(2)]; b_aTg = [self.buf("aTg%d" % i) for i in range(2)]
        hprev = A([44, 2], F32); b_hp = self.buf("hprev")
        raw = [A([514], F32) for _ in range(4)]; b_raw = [self.buf("fraw%d" % i) for i in range(4)]
        p0 = [A([512], F32) for _ in range(4)]; b_p0 = [self.buf("fp0%d" % i) for i in range(4)]
        cv = [A([512], F32) for _ in range(4)]; b_cv = [self.buf("fcv%d" % i) for i in range(4)]
        yout = [A([D], F32) for _ in range(2)]; b_yout = [self.buf("yout%d" % i) for i in range(2)]
        op("dve", lambda e: e.memset(hprev, 0.0), writes=[b_hp])
        an = [0]
        for cg in range(6):
            nch = 4 if cg < 5 else 2
            ch0 = cg * 4
            wua, bwua = wslot()
            wub, bwub = wslot()
            ncol = nch * 128
            op("pool", lambda e, wua=wua, ch0=ch0, ncol=ncol: [
                e.dma_start(out=wua[:, 0:4, 0:ncol], in_=wup_d[:, 0:4, ch0 * 128:ch0 * 128 + ncol]),
                e.dma_start(out=wua[:, 4:8, 0:ncol], in_=wup_d[:, 4:8, ch0 * 128:ch0 * 128 + ncol])], writes=[bwua], dma=2)
            op("pool", lambda e, wub=wub, ch0=ch0, ncol=ncol: [
                e.dma_start(out=wub[:, 0:4, 0:ncol], in_=wup_d[:, 0:4, FFN + ch0 * 128:FFN + ch0 * 128 + ncol]),
                e.dma_start(out=wub[:, 4:8, 0:ncol], in_=wup_d[:, 4:8, FFN + ch0 * 128:FFN + ch0 * 128 + ncol])], writes=[bwub], dma=2)
            wd_, bwd_ = wdg[cg % 2], b_wdg[cg % 2]
            op("pool", lambda e, wd_=wd_, ch0=ch0, nch=nch: e.dma_start(out=wd_[:, 0:nch, :], in_=wdn_d[:, ch0:ch0 + nch, :]),
               writes=[bwd_], dma=1)
            for tt in range(4):
                ag, bag = aTg[an[0] % 2], b_aTg[an[0] % 2]
                an[0] += 1
                for jj in range(nch):
                    for gv in range(2):
                        wu, bwu = (wua, bwua) if gv == 0 else (wub, bwub)
                        bk, bb = self.bank()
                        op("pe", lambda e, bk=bk, wu=wu, jj=jj, tt=tt: [
                            e.matmul(bk[:, 0:512], lhsT=wu[:, k, jj * 128:(jj + 1) * 128], rhs=x1T[:, k, tt * 512:(tt + 1) * 512],
                                     start=(k == 0), stop=(k == 7)) for k in range(8)], reads=[bwu, b_x1T], writes=[bb])
                        bi = (jj % 2) * 2 + gv
                        rw, brw = raw[bi], b_raw[bi]
                        ch = gv * 22 + ch0 + jj
                        op("act", lambda e, rw=rw, ch=ch: e.copy(rw[:, 0:2], hprev[:, ch, :]), reads=[b_hp], writes=[brw])
                        op("act", lambda e, rw=rw, bk=bk: e.copy(rw[:, 2:514], bk[:, 0:512]), reads=[bb], writes=[brw])
                        op("act", lambda e, rw=rw, ch=ch: e.copy(hprev[:, ch, :], rw[:, 512:514]), reads=[brw], writes=[b_hp])
                        p_, bp_ = p0[bi], b_p0[bi]
                        op("act", lambda e, rw=rw, p_=p_, ch=ch: e.mul(p_, rw[:, 0:512], fcw[:, ch, 0:1]), reads=[brw, self.b_lnc], writes=[bp_])
                        c_, bc_ = cv[bi], b_cv[bi]
                        op("dve", lambda e, rw=rw, c_=c_, p_=p_, ch=ch: e.scalar_tensor_tensor(
                            c_, rw[:, 1:513], fcw[:, ch, 1:2], p_, op0=ALU.mult, op1=ALU.add), reads=[brw, self.b_lnc, bp_], writes=[bc_])
                        op("dve", lambda e, rw=rw, c_=c_, ch=ch: e.scalar_tensor_tensor(
                            c_, rw[:, 2:514], fcw[:, ch, 2:3], c_, op0=ALU.mult, op1=ALU.add), reads=[brw, self.b_lnc, bc_], writes=[bc_])
                    cg_, cv_ = (jj % 2) * 2, (jj % 2) * 2 + 1
                    op("act", lambda e, cg_=cg_: e.activation(cv[cg_], cv[cg_], AF.Silu), reads=[b_cv[cg_]], writes=[b_cv[cg_]])
                    op("dve", lambda e, ag=ag, jj=jj, cg_=cg_, cv_=cv_: e.tensor_tensor(ag[:, jj, :], cv[cg_], cv[cv_], op=ALU.mult),
                       reads=[b_cv[cg_], b_cv[cv_]], writes=[bag])
                for s4 in range(4):
                    t = tt * 4 + s4
                    for hf in range(2):
                        bk, bb = self.bank()
                        op("pe", lambda e, bk=bk, ag=ag, wd_=wd_, s4=s4, hf=hf, nch=nch: [
                            e.matmul(bk[:, 0:512], lhsT=ag[:, jj, s4 * 128:(s4 + 1) * 128], rhs=wd_[:, jj, hf * 512:(hf + 1) * 512],
                                     start=(jj == 0), stop=(jj == nch - 1)) for jj in range(nch)], reads=[bag, bwd_], writes=[bb])
                        if cg == 0:
                            op("dve", lambda e, bk=bk, t=t, hf=hf: e.scalar_tensor_tensor(
                                x1[:, t, hf * 512:(hf + 1) * 512], x1[:, t, hf * 512:(hf + 1) * 512], ALPHA, bk[:, 0:512],
                                op0=ALU.mult, op1=ALU.add), reads=[bb, b_x1[t]], writes=[b_x1[t]])
                        else:
                            op("dve", lambda e, bk=bk, t=t, hf=hf: e.tensor_tensor(x1[:, t, hf * 512:(hf + 1) * 512], bk[:, 0:512],
                                                                                     x1[:, t, hf * 512:(hf + 1) * 512], op=ALU.add),
                               reads=[bb, b_x1[t]], writes=[b_x1[t]])
                    if cg == 5:
                        yo, byo = yout[t % 2], b_yout[t % 2]
                        self.ln_tile(x1[:, t, :], lnc[:, 0, :], lnc[:, 1, :], yo, b_x1[t], b_ln2[t % 2], lnt2[t % 2], lst2[t % 2], byo)
                        op("sp", lambda e, yo=yo, t=t: e.dma_start(out=out_d[t * 128:(t + 1) * 128, :], in_=yo), reads=[byo], dma=1,
                           key="outst%d" % (t % 2))


_PROG = {}


def _get_prog(dbg=""):
    if dbg not in _PROG:
        b = Bld(dbg)
        b.build()
        _PROG[dbg] = b
    return _PROG[dbg]


def _in_map(inputs, b, names):
    f = lambda a: np.ascontiguousarray(np.asarray(a, dtype=np.float32))
    x = np.asarray(inputs["x"])[b]
    m = {"xT": f(x.T), "x": f(x), "w_in": f(np.asarray(inputs["w_in"])[0])}
    for k, v in CONST.items():
        m["c_" + k] = v
    pos = np.asarray(inputs["nsa_cmp_pos"])[0].reshape(2, 16, 128)
    m["cmp_pos"] = f(pos.transpose(2, 0, 1))
    m["cmp_w1"] = f(np.asarray(inputs["nsa_cmp_w1"])[0])
    m["cmp_w2"] = f(np.asarray(inputs["nsa_cmp_w2"])[0])
    for n in ("ln1_g", "ln1_b", "ln2_g", "ln2_b"):
        m[n] = f(np.tile(np.asarray(inputs[n])[0][None, :], (128, 1)))
    m["ffn_convw"] = f(np.asarray(inputs["ffn_conv_w"])[0].reshape(3, 44, 128).transpose(2, 1, 0))
    for n in ("w_nsa_out", "w_gdn_out", "w_o", "ffn_w_up", "ffn_w_down"):
        m[n] = f(np.asarray(inputs[n])[0])
    m["gdn_normw"] = f(np.tile(np.asarray(inputs["gdn_norm_w"])[0][None, :], (128, 4)))
    m["gdn_convw"] = f(np.asarray(inputs["gdn_conv_w"])[0].reshape(4, 12, 128).transpose(2, 1, 0))
    m["gdn_dtb"] = f(np.tile(np.asarray(inputs["gdn_dt_bias"])[0][None, :], (128, 16)))
    m["gdn_alog"] = f(np.tile(np.asarray(inputs["gdn_a_log"])[0][None, :], (128, 16)))
    return {k: v for k, v in m.items() if k in names}


def run(inputs, dbg="", cores=8):
    bld = _get_prog(dbg)
    names = set(bld.din.keys())
    in_maps = [_in_map(inputs, b, names) for b in range(cores)]
    res = run_bass_kernel_spmd(bld.nc, in_maps, core_ids=list(range(cores)))
    return res


def kernel(**inputs):
    res = run(inputs, "", 8)
    out = np.stack([np.asarray(r["out"], dtype=np.float32) for r in res.results], axis=0)
    return out
```

```python
import os
from contextlib import ExitStack
import numpy as np
import concourse.bass as bass
import concourse.mybir as mybir
from concourse.bass_utils import run_bass_kernel_spmd

F32 = mybir.dt.float32
BF16 = mybir.dt.bfloat16
AF = mybir.ActivationFunctionType
ALU = mybir.AluOpType
AX = mybir.AxisListType

S = 2048
D = 1024
NT = 16
IN_W = 5408
FFN = 2816
ALPHA = 2.0 ** 0.25
NEGM = -30000.0


class Buf:
    __slots__ = ("name", "w", "r")

    def __init__(self, name):
        self.name = name
        self.w = None
        self.r = {}


class Op:
    __slots__ = ("eng", "fn", "dma", "key", "idx", "deps", "sig", "sigval", "dmaval")


class FW:
    ENGS = ("pe", "act", "dve", "pool", "sp")

    def __init__(self, nc):
        self.nc = nc
        self.ops = []
        self.dma_cum = {}

    def op(self, eng, fn, reads=(), writes=(), dma=0, key=None):
        o = Op()
        o.eng, o.fn, o.dma, o.idx = eng, fn, dma, len(self.ops)
        o.sig, o.sigval, o.dmaval, o.key = False, 0, 0, None
        deps = set()
        raw = set()
        for b in reads:
            if b.w is not None:
                deps.add(b.w)
                raw.add(b.w)
        for b in writes:
            if b.w is not None:
                deps.add(b.w)
            for r in b.r.values():
                deps.add(r)
        keep = []
        for d in deps:
            if d is o:
                continue
            if not d.dma and not dma and d.eng == eng:
                if eng == "pe" or d not in raw:
                    continue
            keep.append(d)
            if not d.dma:
                d.sig = True
        keep.sort(key=lambda d: d.idx)
        o.deps = keep
        if dma:
            if key is None:
                key = writes[0].name if writes else reads[0].name
            o.key = key
            self.dma_cum[key] = self.dma_cum.get(key, 0) + 16 * dma
            o.dmaval = self.dma_cum[key]
        for b in reads:
            b.r[("dma", o.idx) if dma else eng] = o
        for b in writes:
            b.w = o
            b.r = {}
        self.ops.append(o)
        return o

    def emit(self, stack):
        nc = self.nc
        esem = {e: stack.enter_context(nc.semaphore("es_" + e)) for e in self.ENGS}
        dsem = {k: stack.enter_context(nc.semaphore("ds_%d" % i)) for i, k in enumerate(self.dma_cum)}
        cnt = {e: 0 for e in self.ENGS}
        for o in self.ops:
            if not o.dma and o.sig:
                cnt[o.eng] += 1
                o.sigval = cnt[o.eng]
        block = stack.enter_context(nc.Block())

        def run(ename, eng):
            waited = {}
            for o in self.ops:
                if o.eng != ename:
                    continue
                need = {}
                for d in o.deps:
                    if d.dma:
                        s, v = dsem[d.key], d.dmaval
                    else:
                        s, v = esem[d.eng], d.sigval
                    if need.get(s, (None, 0))[1] < v:
                        need[s] = (s, v)
                for s, v in need.values():
                    if waited.get(s, 0) < v:
                        eng.wait_ge(s, v)
                        waited[s] = v
                if o.fn is None:
                    continue
                res = o.fn(eng)
                if not isinstance(res, (list, tuple)):
                    res = [res]
                if o.dma:
                    assert len(res) == o.dma, (len(res), o.dma)
                    for ins in res:
                        ins.then_inc(dsem[o.key], 16)
                elif o.sig:
                    res[-1].then_inc(esem[ename], 1)
            if ename == "sp":
                for k, v in self.dma_cum.items():
                    eng.wait_ge(dsem[k], v)

        @block.tensor
        def _(e):
            run("pe", e)

        @block.scalar
        def _(e):
            run("act", e)

        @block.vector
        def _(e):
            run("dve", e)

        @block.gpsimd
        def _(e):
            run("pool", e)

        @block.sync
        def _(e):
            run("sp", e)


def _consts():
    c = {}
    c["ident"] = np.eye(128, dtype=np.float32)
    t = np.arange(S)
    qa = np.zeros((3, 8, S), np.float32)
    for h in range(8):
        sl = 2.0 ** (-(h + 1))
        qa[0, h] = sl
        qa[1, h] = -sl * (t % 256)
        qa[2, h] = -sl * 256.0 * ((t % 512) // 256)
    c["qaug"] = qa
    ka = np.ones((3, S), np.float32)
    ka[0] = t % 128
    c["kaug"] = ka
    kc = np.ones((3, 128), np.float32)
    kc[0] = 16.0 * np.arange(128)
    c["kcaug"] = kc
    i = np.arange(128)[:, None]
    j = np.arange(128)[None, :]
    m = np.zeros((2, 128, 128), np.float32)
    m[0] = np.where(j >= i, 0.0, NEGM)
    m[1] = np.where(j < i, 0.0, NEGM)
    c["masks"] = m
    n = np.arange(128)[:, None]
    jj = np.arange(512)[None, :]
    cm = np.zeros((4, 128, 512), np.float32)
    for qt in range(4):
        cm[qt] = np.where(qt * 512 + jj >= 16 * n + 31, 0.0, NEGM)
    c["cmask"] = cm
    e = np.zeros((32, 16, 128), np.float32)
    for kt in range(16):
        for ii in range(128):
            e[2 * kt + ii // 64, kt, ii] = 1.0
    c["emat"] = e
    ta = np.zeros((128, 16, 32), np.float32)
    tb = np.zeros((128, 16, 32), np.float32)
    for tt in range(16):
        for p in range(128):
            cur = (tt * 128 + p) // 64
            for b in range(32):
                forced = (b == 0) or (b == cur) or (b == cur - 1)
                if forced:
                    tb[p, tt, b] = 1.0e4
                elif b <= cur:
                    ta[p, tt, b] = 1.0
                else:
                    tb[p, tt, b] = -1.0
    c["topa"] = ta
    c["topb"] = tb
    ov = np.zeros((128, 33), np.float32)
    cs = np.arange(127)[:, None] * 16
    ss = np.arange(32)[None, :] * 64
    ov[:127, :32] = ((cs < ss + 64) & (cs + 32 > ss)).astype(np.float32)
    ov[:127, 32] = 1.0
    c["ovl"] = ov
    g = np.zeros((8, 128, 128), np.float32)
    r = np.arange(128)[:, None]
    cc = np.arange(128)[None, :]
    g[0] = (r <= cc)
    g[1] = (r > cc)
    g[2] = (cc <= r)
    g[3] = (cc < r)
    g[4] = (r <= cc)
    g[5] = 1.0
    g[6] = (cc < r) & ((cc // 64) == (r // 64))
    g[7] = (r >= 64) & (cc < 64)
    c["gmat"] = g
    return c


CONST = _consts()
DBG = os.environ.get("MK_DBG", "")


ARENA_WORDS = 53000


class Bld:
    def __init__(self, dbg=""):
        self.dbg = dbg
        self.nc = bass.Bass("TRN2", target_bir_lowering=False)
        self.fw = FW(self.nc)
        self.st = ExitStack()
        self.bufs = []
        self.off = 0
        self.din = {}
        self.wrot = 0

    def buf(self, name):
        b = Buf(name)
        self.bufs.append(b)
        return b

    def alloc(self, shape, dt):
        n = int(np.prod(shape))
        words = (n + 1) // 2 if dt == BF16 else n
        words = (words + 7) // 8 * 8
        a = self.arena[:, self.off:self.off + words]
        self.off += words
        assert self.off <= ARENA_WORDS, ("SBUF arena overflow", self.off)
        if dt == BF16:
            a = a.bitcast(BF16)[:, 0:n]
        else:
            a = a[:, 0:n]
        if len(shape) == 2:
            a = a.rearrange("p (a b) -> p a b", a=shape[0])
        elif len(shape) == 3:
            a = a.rearrange("p (a b c) -> p a b c", a=shape[0], b=shape[1])
        return a

    def inp(self, name, shape):
        ap = self.nc.dram_tensor(name, list(shape), F32, kind="ExternalInput").ap()
        self.din[name] = ap
        return ap

    def barrier(self):
        deps = set()
        for b in self.bufs:
            if b.w is not None:
                deps.add(b.w)
            for r in b.r.values():
                deps.add(r)
        for e in FW.ENGS:
            o = Op()
            o.eng, o.fn, o.dma, o.idx = e, None, 0, len(self.fw.ops)
            o.sig, o.sigval, o.dmaval, o.key = False, 0, 0, None
            o.deps = [d for d in deps if d.dma or d.eng != e]
            for d in o.deps:
                if not d.dma:
                    d.sig = True
            self.fw.ops.append(o)

    def bank(self):
        i = self.brot % len(self.banks_rot)
        self.brot += 1
        return self.banks_rot[i]

    def dma(self, eng, out, in_, reads, writes, key=None):
        self.fw.op(eng, lambda e: e.dma_start(out=out, in_=in_), reads=reads, writes=writes, dma=1, key=key)

    def ldw(self, src3, ncols, reads=()):
        i = self.wrot % len(self.wb)
        self.wrot += 1
        ap, b = self.wb[i]
        dst = ap[:, :, 0:ncols]
        self.fw.op("pool", lambda e: [e.dma_start(out=dst[:, 0:4, :], in_=src3[:, 0:4, :]),
                                      e.dma_start(out=dst[:, 4:8, :], in_=src3[:, 4:8, :])],
                   reads=list(reads), writes=[b], dma=2)
        return ap, b

    def load_xT(self):
        A, op = self.alloc, self.fw.op
        xT = A([8, S], BF16)
        b_xT = [self.buf("xT%d" % k) for k in range(8)]
        for k in range(8):
            op("pool", lambda e, k=k: e.dma_start(out=xT[:, k, :], in_=self.din["xT"][k * 128:(k + 1) * 128, :]),
               writes=[b_xT[k]], dma=1, key="xT%d" % k)
        self.xT, self.b_xT = xT, b_xT
        self.wb = [(A([8, 512], BF16), self.buf("wb%d" % i)) for i in range(3)]

    def evac(self, out, in_, reads, writes, scale=None):
        op = self.fw.op
        self.evn += 1
        if self.evn % 2 == 0:
            if scale is None:
                op("act", lambda e: e.copy(out, in_), reads=reads, writes=writes)
            else:
                op("act", lambda e: e.mul(out, in_, scale), reads=reads, writes=writes)
        else:
            if scale is None:
                op("dve", lambda e: e.tensor_copy(out, in_), reads=reads, writes=writes)
            else:
                op("dve", lambda e: e.tensor_scalar_mul(out, in_, scale), reads=reads, writes=writes)

    def proj_fm(self, wap, wbuf, c0, M, dst_fn, dbufs, scale=None, evac=None):
        op, xT, b_xT = self.fw.op, self.xT, self.b_xT
        for tt in range(4):
            bk, bb = self.bank()
            op("pe", lambda e, bk=bk, tt=tt: [
                e.matmul(bk[0:M, 0:512], lhsT=wap[:, k, c0:c0 + M], rhs=xT[:, k, tt * 512:(tt + 1) * 512],
                         start=(k == 0), stop=(k == 7)) for k in range(8)],
               reads=[wbuf] + b_xT, writes=[bb])
            if evac is None:
                self.evac(dst_fn(tt), bk[0:M, 0:512], [bb], dbufs, scale)
            else:
                evac(tt, bk, bb)

    def build(self):
        nc, fw = self.nc, self.fw
        st = self.st
        op = fw.op
        inp = self.inp
        inp("xT", [D, S])
        inp("x", [S, D])
        w_in = inp("w_in", [D, IN_W])
        self.w_in3 = w_in.rearrange("(k p) n -> p k n", p=128)
        self.cd = {k: inp("c_" + k, v.shape) for k, v in CONST.items()}
        inp("cmp_pos", [128, 2, 16])
        inp("cmp_w1", [2, 2048, 64])
        inp("cmp_w2", [2, 64, 64])
        inp("gdn_normw", [128, 512])
        for n in ("ln1_g", "ln1_b", "ln2_g", "ln2_b"):
            inp(n, [128, D])
        inp("ffn_convw", [128, 44, 3])
        inp("w_nsa_out", [512, D]); inp("w_gdn_out", [512, D]); inp("w_o", [D, D])
        inp("ffn_w_up", [D, 2 * FFN]); inp("ffn_w_down", [FFN, D])
        inp("gdn_convw", [128, 12, 4])
        inp("gdn_dtb", [128, 64])
        inp("gdn_alog", [128, 64])
        self.out_d = nc.dram_tensor("out", [S, D], F32, kind="ExternalOutput").ap()
        self.dbg_d = None
        if self.dbg:
            self.dbg_d = nc.dram_tensor("dbg", [S, 512], F32, kind="ExternalOutput").ap()
        self.arena = st.enter_context(nc.sbuf_tensor("arena", [128, ARENA_WORDS], F32))
        banks = []
        for i in range(8):
            t = st.enter_context(nc.psum_tensor("bank%d" % i, [128, 512], F32))
            banks.append((t, self.buf("bank%d" % i)))
        self.banks = banks
        self.banks_rot = banks[4:8]
        self.brot = 0
        self.evn = 0
        A = self.alloc
        self.ident = A([128], BF16)
        self.identf = A([128], F32)
        self.zeros = A([512], BF16)
        self.b_c = self.buf("consts")
        self.GT = A([NT, 32], F32); self.b_GT = self.buf("GT")
        self.oy = A([8, S], BF16)
        self.onsaT = self.oy[:, 0:4, :]; self.b_onsaT = self.buf("onsaT")
        self.ygT = self.oy[:, 4:8, :]; self.b_ygT = self.buf("ygT")
        op("pool", lambda e: e.dma_start(out=self.ident, in_=self.cd["ident"]), writes=[self.b_c], dma=1, key="c0")
        op("sp", lambda e: e.dma_start(out=self.identf, in_=self.cd["ident"]), writes=[self.b_c], dma=1, key="c1")
        op("dve", lambda e: e.memset(self.zeros, 0.0), writes=[self.b_c])
        mark = self.off
        self.p1_nsa()
        if self.dbg == "nsa":
            fw.emit(st)
            return
        self.barrier()
        self.off = mark
        self.p2_gdn()
        if self.dbg == "gdn":
            fw.emit(st)
            return
        self.barrier()
        self.off = mark
        self.p34()
        fw.emit(st)

    def dump_tok(self, src_fn, bufs, ncol=512):
        A, op = self.alloc, self.fw.op
        stg = A([512], F32); b_stg = self.buf("stg")
        for t in range(NT):
            op("dve", lambda e, t=t: e.tensor_copy(stg[:, 0:ncol], src_fn(t)), reads=bufs, writes=[b_stg])
            op("sp", lambda e, t=t: e.dma_start(out=self.dbg_d[t * 128:(t + 1) * 128, 0:ncol], in_=stg[:, 0:ncol]),
               reads=[b_stg], dma=1, key="dbgo")

    def p1_nsa(self):
        nc, fw, op, A = self.nc, self.fw, self.fw.op, self.alloc
        banks = self.banks
        ident, identf, zeros, b_c = self.ident, self.identf, self.zeros, self.b_c
        GT, b_GT, onsaT, b_onsaT = self.GT, self.b_GT, self.onsaT, self.b_onsaT
        cd, w_in3 = self.cd, self.w_in3
        self.load_xT()
        xT, b_xT = self.xT, self.b_xT
        QA = A([8, S], BF16); b_QA = [self.buf("QA%d" % h) for h in range(8)]
        KA = A([2, S], BF16); b_KA = [self.buf("KA%d" % i) for i in range(2)]
        off_ct = self.off
        CT = A([4, S], BF16); b_ct1 = self.buf("CT"); b_CT = [b_ct1] * 4
        self.off = off_ct
        otok = A([NT, 512], BF16); b_otok = b_ct1
        VA = A([2, NT, 65], BF16); b_VA = self.buf("VA")
        PT = [(A([512], BF16), self.buf("PT%d" % i)) for i in range(6)]
        cmask = A([4, 512], BF16)
        masks = A([2, 128], BF16)
        emat = A([16, 128], BF16)
        topa = A([NT, 32], F32)
        topb = A([NT, 32], F32)
        ovl = A([33], BF16)
        b_k = self.buf("nsa_consts")
        selT = A([S], BF16); b_selT = self.buf("selT")
        kcA = A([2, 128], BF16); b_kcA = self.buf("kcA")
        vcA = A([2, 65], BF16); b_vcA = self.buf("vcA")
        w1 = A([2, 32, 64], BF16)
        w1b = A([2, 16, 64], BF16)
        posb = A([2, 16], BF16)
        w2 = A([2, 64], BF16)
        b_cw = self.buf("cmp_w")
        gl = A([4, 128], BF16); b_gl = self.buf("gl")
        ctmp = A([6, 128], F32); b_ctmp = self.buf("ctmp")
        cbias = A([2], F32); b_cbias = self.buf("cbias")
        tacc = A([4, 4, 64], F32); b_tacc = self.buf("tacc")
        sm = A([4, 64], F32); b_sm = self.buf("sm")

        def cld(dst, src, key, eng="pool"):
            op(eng, lambda e: e.dma_start(out=dst, in_=src), writes=[b_k], dma=1, key=key)
        wk, bwk = self.ldw(w_in3[:, :, 512:768], 256)
        op("pool", lambda e: [e.dma_start(out=cmask, in_=cd["cmask"].rearrange("q p j -> p q j")),
                              e.dma_start(out=masks, in_=cd["masks"].rearrange("m p j -> p m j")),
                              e.dma_start(out=emat[0:32], in_=cd["emat"]),
                              e.dma_start(out=ovl, in_=cd["ovl"])], writes=[b_k], dma=4, key="k0")
        op("sp", lambda e: [e.dma_start(out=topa, in_=cd["topa"]), e.dma_start(out=topb, in_=cd["topb"])],
           writes=[b_k], dma=2, key="k1")
        for i in range(2):
            op("pool", lambda e, i=i: e.dma_start(out=KA[64:67, i, :], in_=cd["kaug"]),
               writes=[b_KA[i]], dma=1, key="ka%d" % i)
        for h in range(8):
            op("pool", lambda e, h=h: e.dma_start(out=QA[64:67, h, :], in_=cd["qaug"][:, h, :]),
               writes=[b_QA[h]], dma=1, key="qa%d" % h)
        for g in range(2):
            op("pool", lambda e, g=g: e.dma_start(out=kcA[64:67, g, :], in_=cd["kcaug"]),
               writes=[b_kcA], dma=1, key="kc%d" % g)
        w1_d, w2_d, pos_d = self.din["cmp_w1"], self.din["cmp_w2"], self.din["cmp_pos"]
        op("pool", lambda e: [e.dma_start(out=w1[0:64], in_=w1_d.rearrange("i (l d) h -> d i l h", d=64)),
                              e.dma_start(out=w1b, in_=w1_d.rearrange("i (c p) h -> p i c h", p=128)),
                              e.dma_start(out=posb, in_=pos_d),
                              e.dma_start(out=w2[0:64], in_=w2_d.rearrange("i h d -> h i d"))],
           writes=[b_cw], dma=4, key="cw0")
        op("dve", lambda e: e.memset(VA[:, :, :, 64:65], 1.0), writes=[b_VA])
        op("dve", lambda e: e.memset(vcA[:, :, 64:65], 1.0), writes=[b_vcA])

        for i in range(4):
            self.proj_fm(wk, bwk, i * 64, 64, lambda tt, i=i: CT[0:64, i, tt * 512:(tt + 1) * 512], [b_CT[i]])

        bkc, bbc = banks[4]
        op("pe", lambda e: [e.matmul(bkc[0:64, i:i + 1], lhsT=w1b[:, i, c, :], rhs=posb[:, i, c:c + 1],
                                     start=(c == 0), stop=(c == 15)) for i in range(2) for c in range(16)],
           reads=[b_cw], writes=[bbc])
        op("dve", lambda e: e.tensor_copy(cbias[0:64, :], bkc[0:64, 0:2]), reads=[bbc], writes=[b_cbias])
        bk2, bb2 = banks[5]
        for ci in range(4):
            i = ci // 2
            op("pe", lambda e, ci=ci, i=i: [
                e.matmul(bk2[0:64, ci * 128:ci * 128 + 127], lhsT=w1[0:64, i, l, :],
                         rhs=CT[0:64, ci, l:l + 16 * 126 + 1:16], start=(l == 0), stop=(l == 31)) for l in range(32)],
               reads=[b_cw, b_CT[ci]], writes=[bb2])
        for ci in range(4):
            i = ci // 2
            uu = ctmp[0:64, ci, 0:127]
            op("act", lambda e, ci=ci, i=i, uu=uu: e.activation(uu, bk2[0:64, ci * 128:ci * 128 + 127], AF.Identity,
                                                                bias=cbias[0:64, i:i + 1]),
               reads=[bb2, b_cbias], writes=[b_ctmp])
            a2 = ctmp[0:64, 4, 0:127]
            a3 = ctmp[0:64, 5, 0:127]
            op("dve", lambda e, uu=uu, a2=a2: e.tensor_tensor(a2, uu, uu, op=ALU.mult), reads=[b_ctmp], writes=[b_ctmp])
            op("dve", lambda e, a2=a2: e.tensor_scalar(a2, a2, 0.044715, 1.0, op0=ALU.mult, op1=ALU.add),
               reads=[b_ctmp], writes=[b_ctmp])
            op("dve", lambda e, uu=uu, a2=a2: e.tensor_tensor(a2, a2, uu, op=ALU.mult), reads=[b_ctmp], writes=[b_ctmp])
            op("act", lambda e, a2=a2, a3=a3: e.activation(a3, a2, AF.Sigmoid, scale=1.5957691216057308),
               reads=[b_ctmp], writes=[b_ctmp])
            op("dve", lambda e, ci=ci, uu=uu, a3=a3: e.tensor_tensor(gl[0:64, ci, 0:127], uu, a3, op=ALU.mult),
               reads=[b_ctmp], writes=[b_gl])
        bk3, bb3 = banks[6]
        for g in range(2):
            op("pe", lambda e, g=g: e.matmul(bk3[0:64, g * 128:g * 128 + 127], lhsT=w2[0:64, 0, :], rhs=gl[0:64, g, 0:127],
                                             start=True, stop=True), reads=[b_cw, b_gl], writes=[bb3])
            op("dve", lambda e, g=g: e.tensor_copy(kcA[0:64, g, 0:127], bk3[0:64, g * 128:g * 128 + 127]),
               reads=[bb3], writes=[b_kcA])
            op("pe", lambda e, g=g: e.matmul(bk3[0:127, 256 + g * 64:256 + (g + 1) * 64], lhsT=gl[0:64, 2 + g, 0:127],
                                             rhs=w2[0:64, 1, :], start=True, stop=True), reads=[b_cw, b_gl], writes=[bb3])
            op("dve", lambda e, g=g: e.tensor_copy(vcA[0:127, g, 0:64], bk3[0:127, 256 + g * 64:256 + (g + 1) * 64]),
               reads=[bb3], writes=[b_vcA])

        STOP = int(os.environ.get("MK_STOP", "0"))
        if STOP == 1:
            return
        if STOP == 6:
            stg = A([512], F32); b_stg = self.buf("stg")
            for i6, (src, np_, nc_) in enumerate([(kcA[0:67].rearrange("p a b -> p (a b)"), 67, 256),
                                                (vcA[0:127].rearrange("p a b -> p (a b)"), 127, 130),
                                                (gl[0:64].rearrange("p a b -> p (a b)"), 64, 512)]):
                op("dve", lambda e, src=src, np_=np_, nc_=nc_: e.tensor_copy(stg[0:np_, 0:nc_], src),
                   reads=[b_kcA, b_vcA, b_gl], writes=[b_stg])
                op("sp", lambda e, i6=i6, np_=np_, nc_=nc_: e.dma_start(out=self.dbg_d[i6 * 128:i6 * 128 + np_, 0:nc_], in_=stg[0:np_, 0:nc_]),
                   reads=[b_stg], dma=1, key="dbgo")
            return
        acc = banks[0:4]
        scb = banks[4]
        sbanks = banks[5:8]
        rot = {"s": 0, "p": 0}
        slopes = [2.0 ** (-(h + 1)) for h in range(8)]

        def zero_bank(bk, bb, ncol):
            op("pe", lambda e: e.matmul(bk[:, 0:ncol], lhsT=zeros[:, 0:128], rhs=zeros[:, 0:ncol], start=True, stop=True),
               reads=[b_c], writes=[bb])

        def make_items(g, qt, br):
            q0 = qt * 512
            items = []
            for hl in range(4):
                h = g * 4 + hl
                if br == 0:
                    items.append(dict(h=h, hl=hl, kind="cmp", K=127, c0=0, c1=512, bias=-slopes[h] * (q0 - 31)))
                elif br == 2:
                    for m in range(8):
                        k0 = q0 - 512 + 128 * m
                        if k0 < 0:
                            continue
                        if m < 4:
                            c0, c1, mk, mc = 0, 128 * (m + 1), 1, 128 * m
                        else:
                            c0, c1, mk, mc = 128 * (m - 4), 512, 0, 128 * (m - 4)
                        items.append(dict(h=h, hl=hl, kind="win", K=128, c0=c0, c1=c1, kt=k0 // 128, mk=mk, mc=mc,
                                          bias=-slopes[h] * (q0 - k0)))
                else:
                    for kt in range(0, (q0 + 512) // 128):
                        k0 = kt * 128
                        if k0 < q0:
                            c0, c1, mk, mc = 0, 512, None, 0
                        else:
                            mm = (k0 - q0) // 128
                            c0, c1, mk, mc = 128 * mm, 512, 0, 128 * mm
                        items.append(dict(h=h, hl=hl, kind="slc", K=128, c0=c0, c1=c1, kt=kt, mk=mk, mc=mc,
                                          bias=-slopes[h] * (q0 - k0)))
            return items

        def emit_S(g, qt, it):
            q0 = qt * 512
            bk, bb = sbanks[rot["s"] % 3]
            rot["s"] += 1
            pt, pb = PT[rot["p"] % 6]
            rot["p"] += 1
            it["pt"], it["pb"] = pt, pb
            hl, K, c0, c1 = it["hl"], it["K"], it["c0"], it["c1"]
            kind = it["kind"]

            def pe_fn(e):
                ins = []
                if kind == "cmp":
                    ins.append(e.matmul(bk[0:127, 0:512], lhsT=kcA[0:67, g, 0:127], rhs=QA[0:67, it["h"], q0:q0 + 512],
                                        start=True, stop=False))
                    ins.append(e.matmul(bk[0:127, 0:512], lhsT=ident[0:127, 0:127], rhs=cmask[0:127, qt, :],
                                        start=False, stop=True))
                else:
                    ki = 0 if kind == "win" else 1
                    kt = it["kt"]
                    more = (it["mk"] is not None) or kind == "slc"
                    ins.append(e.matmul(bk[:, c0:c1], lhsT=KA[0:67, ki, kt * 128:(kt + 1) * 128],
                                        rhs=QA[0:67, it["h"], q0 + c0:q0 + c1], start=True, stop=not more))
                    if kind == "slc":
                        ins.append(e.matmul(bk[:, c0:c1], lhsT=emat[0:32, kt, :], rhs=selT[0:32, q0 + c0:q0 + c1],
                                            start=False, stop=it["mk"] is None))
                    if it["mk"] is not None:
                        mc = it["mc"]
                        ins.append(e.matmul(bk[:, mc:mc + 128], lhsT=ident[:, :], rhs=masks[:, it["mk"], :],
                                            start=False, stop=True))
                return ins
            rd = [b_QA[it["h"]], b_k, b_c]
            if kind == "cmp":
                rd.append(b_kcA)
            elif kind == "win":
                rd.append(b_KA[0])
            else:
                rd += [b_KA[1], b_selT]
            op("pe", pe_fn, reads=rd, writes=[bb])
            op("act", lambda e: e.activation(pt[0:K, c0:c1], bk[0:K, c0:c1], AF.Exp, bias=float(it["bias"])),
               reads=[bb], writes=[pb])

        def emit_PV(g, qt, it):
            pt, pb = it["pt"], it["pb"]
            K, c0, c1, hl, kind = it["K"], it["c0"], it["c1"], it["hl"], it["kind"]

            def pe_fn(e):
                ins = []
                for qs in range(c0 // 128, c1 // 128):
                    if kind == "cmp":
                        rhs = vcA[0:127, g, :]
                    else:
                        vi = 1 if kind == "win" else 0
                        rhs = VA[:, vi, it["kt"], :]
                    ins.append(e.matmul(acc[qs][0][:, hl * 65:(hl + 1) * 65], lhsT=pt[0:K, qs * 128:(qs + 1) * 128],
                                        rhs=rhs, start=False, stop=True, skip_group_check=True))
                    if kind == "cmp":
                        ins.append(e.matmul(scb[0][:, (qs * 4 + hl) * 32:(qs * 4 + hl + 1) * 32],
                                            lhsT=pt[0:127, qs * 128:(qs + 1) * 128], rhs=ovl[0:127, 0:32],
                                            start=False, stop=True, skip_group_check=True))
                return ins
            wr = [acc[qs][1] for qs in range(c0 // 128, c1 // 128)]
            if kind == "cmp":
                wr.append(scb[1])
            op("pe", pe_fn, reads=[pb, b_VA, b_vcA, b_k], writes=wr)

        def post(g, qt, br):
            for qs in range(4):
                t = qt * 4 + qs
                ab, abuf = acc[qs]
                lm = sm[:, qs, 0:4]
                rl = sm[:, qs, 4:8]
                ff = sm[:, qs, 8:12]
                a3 = ab[:, 0:260].rearrange("p (h d) -> p h d", h=4)
                op("dve", lambda e, lm=lm, a3=a3: e.tensor_scalar_max(lm, a3[:, :, 64], 1e-30), reads=[abuf], writes=[b_sm])
                op("dve", lambda e, lm=lm, rl=rl: e.reciprocal(rl, lm), reads=[b_sm], writes=[b_sm])
                gc = br * 8 + g * 4
                op("dve", lambda e, rl=rl, ff=ff, t=t, gc=gc: e.tensor_tensor(ff, rl, GT[:, t, gc:gc + 4], op=ALU.mult),
                   reads=[b_sm, b_GT], writes=[b_sm])
                for hl in range(4):
                    if br == 0:
                        op("dve", lambda e, hl=hl, qs=qs, a3=a3, ff=ff: e.tensor_scalar_mul(tacc[:, qs, hl, :], a3[:, hl, 0:64],
                                                                                          ff[:, hl:hl + 1]),
                           reads=[abuf, b_sm], writes=[b_tacc])
                    else:
                        last = (br == 1)
                        dst = otok[:, t, (g * 4 + hl) * 64:(g * 4 + hl + 1) * 64] if last else tacc[:, qs, hl, :]
                        op("dve", lambda e, hl=hl, qs=qs, a3=a3, ff=ff, dst=dst: e.scalar_tensor_tensor(
                            dst, a3[:, hl, 0:64], ff[:, hl:hl + 1], tacc[:, qs, hl, :], op0=ALU.mult, op1=ALU.add),
                           reads=[abuf, b_sm, b_tacc], writes=[b_otok if last else b_tacc])
                if br == 0:
                    sc = sm[:, qs, 16:48]
                    s4 = scb[0][:, qs * 128:(qs + 1) * 128].rearrange("p (h b) -> p h b", h=4)
                    op("dve", lambda e, sc=sc, s4=s4, rl=rl: e.tensor_scalar_mul(sc, s4[:, 0, :], rl[:, 0:1]),
                       reads=[scb[1], b_sm], writes=[b_sm])
                    for hl in range(1, 4):
                        op("dve", lambda e, sc=sc, s4=s4, rl=rl, hl=hl: e.scalar_tensor_tensor(
                            sc, s4[:, hl, :], rl[:, hl:hl + 1], sc, op0=ALU.mult, op1=ALU.add),
                           reads=[scb[1], b_sm], writes=[b_sm])
                    op("dve", lambda e, sc=sc, t=t: e.tensor_tensor(sc, sc, topa[:, t, :], op=ALU.mult),
                       reads=[b_sm, b_k], writes=[b_sm])
                    op("dve", lambda e, sc=sc, t=t: e.tensor_tensor(sc, sc, topb[:, t, :], op=ALU.add),
                       reads=[b_sm, b_k], writes=[b_sm])
                    t8 = sm[:, qs, 48:56]
                    op("dve", lambda e, sc=sc, t8=t8: e.max(t8, sc), reads=[b_sm], writes=[b_sm])
                    op("dve", lambda e, sc=sc, t8=t8: e.tensor_scalar(sc, sc, t8[:, 7:8], -NEGM, op0=ALU.is_ge, op1=ALU.mult),
                       reads=[b_sm], writes=[b_sm])
                    sng = gl[:, qs, 0:32]
                    op("dve", lambda e, sc=sc, sng=sng: e.tensor_scalar_add(sng, sc, NEGM), reads=[b_sm], writes=[b_gl])
                    tb_, tbb = sbanks[rot["s"] % 3]
                    rot["s"] += 1
                    tv = tb_.bitcast(BF16)
                    op("pe", lambda e, tv=tv, sng=sng: e.transpose(tv[0:32, 0:128], sng, ident[:, :]),
                       reads=[b_gl, b_c], writes=[tbb])
                    op("act", lambda e, tv=tv, t=t: e.copy(selT[0:32, t * 128:(t + 1) * 128], tv[0:32, 0:128]),
                       reads=[tbb], writes=[b_selT])

        for g in range(2):
            wq, bwq = self.ldw(w_in3[:, :, g * 256:(g + 1) * 256], 256)
            for hl in range(4):
                self.proj_fm(wq, bwq, hl * 64, 64, lambda tt, hq=g * 4 + hl: QA[0:64, hq, tt * 512:(tt + 1) * 512], [b_QA[g * 4 + hl]], 0.125)
            if STOP == 21:
                return
            i = self.wrot % 3
            self.wrot += 1
            wv, bwv = self.wb[i]
            ncv = 160 if g == 0 else 128
            if int(os.environ.get("MK_VAR", "0")) == 3:
                ncv = 256
            def wload(e, g=g, wv=wv):
                ins = [e.dma_start(out=wv[:, :, 0:64], in_=w_in3[:, :, 896 + g * 64:960 + g * 64]),
                       e.dma_start(out=wv[:, :, 64:128], in_=w_in3[:, :, 1152 + g * 64:1216 + g * 64]),
                       e.dma_start(out=wv[:, :, 256:320], in_=w_in3[:, :, 1024 + g * 64:1088 + g * 64]),
                       e.dma_start(out=wv[:, :, 320:384], in_=w_in3[:, :, 768 + g * 64:832 + g * 64])]
                if g == 0:
                    ins.append(e.dma_start(out=wv[:, :, 128:152], in_=w_in3[:, :, 1280:1304]))
                    ins.append(e.dma_start(out=wv[:, :, 152:160], in_=w_in3[:, :, 2840:2848]))
                return ins
            op("pool", wload, writes=[bwv], dma=6 if g == 0 else 4)
            for i2 in range(2):
                self.proj_fm(wv, bwv, 256 + i2 * 64, 64, lambda tt, i2=i2: KA[0:64, i2, tt * 512:(tt + 1) * 512], [b_KA[i2]])
            if STOP == 22:
                return
            for t in range(int(os.environ.get("MK_NT", "16"))):
                bk, bb = self.bank()
                op("pe", lambda e, bk=bk, t=t, wv=wv, ncv=ncv: [
                    e.matmul(bk[:, 0:ncv], lhsT=xT[:, k, t * 128:(t + 1) * 128], rhs=wv[:, k, 0:ncv],
                             start=(k == 0), stop=(k == 7)) for k in range(8)],
                   reads=[bwv] + b_xT, writes=[bb])
                VAR = int(os.environ.get("MK_VAR", "0"))
                if VAR not in (1, 4):
                    self.evac(VA[:, :, t, 0:64], bk[:, 0:128].rearrange("p (v d) -> p v d", v=2), [bb], [b_VA])
                if g == 0 and VAR not in (2, 4):
                    self.evac(GT[:, t, :], bk[:, 128:160], [bb], [b_GT])
            if STOP == 23:
                return
            if g == 0:
                op("act", lambda e: e.activation(GT[:, :, 0:28], GT[:, :, 0:28], AF.Sigmoid), reads=[b_GT], writes=[b_GT])
                op("act", lambda e: e.activation(cbias[0:64, 0:1], cbias[0:64, 1:2], AF.Exp), reads=[b_cbias], writes=[b_ctmp])
            if STOP == 2 or (STOP == 4 and g == 1):
                return
            if int(os.environ.get("MK_BAR", "0")) and g == 0:
                self.barrier()
            for qt in range(4):
                for br in (0, 2, 1):
                    for qs in range(4):
                        zero_bank(acc[qs][0], acc[qs][1], 260)
                    if br == 0:
                        zero_bank(scb[0], scb[1], 512)
                    items = make_items(g, qt, br)
                    for i, it in enumerate(items):
                        emit_S(g, qt, it)
                        if i >= 3:
                            emit_PV(g, qt, items[i - 3])
                    for it in items[max(0, len(items) - 3):]:
                        emit_PV(g, qt, it)
                    post(g, qt, br)
            if STOP == 3:
                return
            if STOP == 9:
                stg = A([512], F32); b_stg = self.buf("stg")
                for t in range(2):
                    op("dve", lambda e, t=t: e.tensor_copy(stg, otok[:, t, :]), reads=[b_otok], writes=[b_stg])
                    op("sp", lambda e, t=t: e.dma_start(out=self.dbg_d[t * 128:(t + 1) * 128, :], in_=stg), reads=[b_stg], dma=1, key="dbgo")
                op("dve", lambda e: e.tensor_copy(stg[0:32], selT[0:32, 0:512]), reads=[b_selT], writes=[b_stg])
                op("sp", lambda e: e.dma_start(out=self.dbg_d[256:288, :], in_=stg[0:32]), reads=[b_stg], dma=1, key="dbgo")
                op("dve", lambda e: e.tensor_copy(stg[:, 0:128], GT.rearrange("p t c -> p (t c)")[:, 0:128]), reads=[b_GT], writes=[b_stg])
                op("sp", lambda e: e.dma_start(out=self.dbg_d[384:512, 0:128], in_=stg[:, 0:128]), reads=[b_stg], dma=1, key="dbgo")
                return
            if STOP == 7:
                stg = A([512], F32); b_stg = self.buf("stg")
                lst = [(cmask[0:127, 0, :], 127, 512), (vcA[0:127].rearrange("p a b -> p (a b)"), 127, 130),
                       (kcA[0:67, :, 0:127], 67, 254), (ovl[0:127, 0:33], 127, 33), (GT[:, 0, :], 128, 32), (selT[0:32, 0:512], 32, 512)]
                for i6, (src, np_, nc_) in enumerate(lst):
                    dst = stg[0:np_, 0:nc_]
                    if i6 == 2:
                        dst = dst.rearrange("p (a b) -> p a b", a=2)
                    op("dve", lambda e, src=src, dst=dst: e.tensor_copy(dst, src),
                       reads=[b_kcA, b_vcA, b_k, b_GT, b_selT], writes=[b_stg])
                    op("sp", lambda e, i6=i6, np_=np_, nc_=nc_: e.dma_start(out=self.dbg_d[i6 * 128:i6 * 128 + np_, 0:nc_], in_=stg[0:np_, 0:nc_]),
                       reads=[b_stg], dma=1, key="dbgo")
                return
            if STOP == 5:
                stg = A([512], F32); b_stg = self.buf("stg")
                srcs = [KA[:, 0, 0:512], KA[:, 1, 0:512]] + [QA[:, hl, 0:512] for hl in range(4)]
                for i5, src in enumerate(srcs):
                    op("dve", lambda e, src=src: e.tensor_copy(stg[0:67], src[0:67]), reads=b_KA + b_QA, writes=[b_stg])
                    op("sp", lambda e, i5=i5: e.dma_start(out=self.dbg_d[i5 * 128:i5 * 128 + 67, :], in_=stg[0:67]),
                       reads=[b_stg], dma=1, key="dbgo")
                return

        if self.dbg == "nsa":
            self.dump_tok(lambda t: otok[:, t, :], [b_otok])
            return
        for t in range(NT):
            tb_, tbb = sbanks[rot["s"] % 3]
            rot["s"] += 1
            tv = tb_.bitcast(BF16)
            op("pe", lambda e, tv=tv, t=t: [e.transpose(tv[:, c * 128:(c + 1) * 128], otok[:, t, c * 128:(c + 1) * 128], ident[:, :])
                                            for c in range(4)], reads=[b_otok, b_c], writes=[tbb])
            op("act", lambda e, tv=tv, t=t: e.copy(onsaT[:, :, t * 128:(t + 1) * 128],
                                                   tv[:, 0:512].rearrange("p (c j) -> p c j", c=4)),
               reads=[tbb], writes=[b_onsaT])

    def p2_gdn(self):
        nc, fw, op, A = self.nc, self.fw, self.fw.op, self.alloc
        ident, identf, b_c = self.ident, self.identf, self.b_c
        GT, b_GT, ygT, b_ygT = self.GT, self.b_GT, self.ygT, self.b_ygT
        cd, w_in3 = self.cd, self.w_in3
        din = self.din
        self.banks_rot = self.banks
        QKV = A([12, S], BF16); b_QKV = [self.buf("QKV%d" % j) for j in range(12)]
        GS = A([NT, 512], BF16); b_GS = self.buf("GS")
        gmat = A([8, 128], F32); b_gm = self.buf("gmat")
        onesb = A([128], BF16)
        ident4 = A([4, 128], BF16)
        normw = A([512], F32)
        convw = A([12, 4], F32)
        dtb = A([64], F32); alog = A([64], F32)
        sc = {n: A([64], F32) for n in ("g", "gcum", "gtot", "eg", "egt", "kds", "bge", "nb", "tmp")}
        b_sc = self.buf("gscal")
        op("sp", lambda e: e.dma_start(out=gmat, in_=cd["gmat"].rearrange("m p j -> p m j")), writes=[b_gm], dma=1, key="g0")
        op("sp", lambda e: e.dma_start(out=normw, in_=din["gdn_normw"]), writes=[b_gm], dma=1, key="g1")
        op("sp", lambda e: e.dma_start(out=convw, in_=din["gdn_convw"]), writes=[b_gm], dma=1, key="g2")
        op("sp", lambda e: e.dma_start(out=dtb, in_=din["gdn_dtb"]), writes=[b_gm], dma=1, key="g3")
        op("sp", lambda e: e.dma_start(out=alog, in_=din["gdn_alog"]), writes=[b_gm], dma=1, key="g4")
        op("dve", lambda e: e.memset(onesb, 1.0), writes=[b_gm])
        op("dve", lambda e: [e.tensor_copy(ident4[:, h, :], ident) for h in range(4)], reads=[b_c], writes=[b_gm])
        Tri, SU, Mli, Mls, MTi, onesf, Mbd, M21 = (gmat[:, i, :] for i in range(8))

        GTf = GT.rearrange("p t c -> p (t c)")
        g_, gcum, gtot, eg, egt, kds, bge, nb, tmp = (sc[n] for n in ("g", "gcum", "gtot", "eg", "egt", "kds", "bge", "nb", "tmp"))
        v3 = lambda a: a.rearrange("p (t h) -> p t h", h=4)
        op("dve", lambda e: e.tensor_tensor(v3(tmp), GT[:, :, 28:32], v3(dtb), op=ALU.add), reads=[b_GT, b_gm], writes=[b_sc])
        a1, a2, a3, a4, a5 = gcum, gtot, eg, egt, kds
        S_ = lambda fn: op("dve", fn, reads=[b_sc], writes=[b_sc])
        S_(lambda e: e.tensor_scalar_mul(a1, tmp, -1.0))
        S_(lambda e: e.tensor_tensor(a1, a1, tmp, op=ALU.max))
        op("act", lambda e: e.activation(a1, a1, AF.Exp, scale=-1.0), reads=[b_sc], writes=[b_sc])
        S_(lambda e: e.tensor_scalar_add(a2, a1, 2.0))
        S_(lambda e: e.reciprocal(a2, a2))
        S_(lambda e: e.tensor_tensor(a2, a2, a1, op=ALU.mult))
        S_(lambda e: e.tensor_tensor(a3, a2, a2, op=ALU.mult))
        S_(lambda e: e.tensor_scalar(a4, a3, 1.0 / 9.0, 1.0 / 7.0, op0=ALU.mult, op1=ALU.add))
        for cst in (1.0 / 5.0, 1.0 / 3.0, 1.0):
            S_(lambda e: e.tensor_tensor(a4, a4, a3, op=ALU.mult))
            S_(lambda e, cst=cst: e.tensor_scalar_add(a4, a4, cst))
        S_(lambda e: e.tensor_tensor(a4, a4, a2, op=ALU.mult))
        S_(lambda e: e.tensor_scalar_max(a5, tmp, 0.0))
        S_(lambda e: e.scalar_tensor_tensor(tmp, a4, 2.0, a5, op0=ALU.mult, op1=ALU.add))
        op("act", lambda e: e.activation(g_, alog, AF.Exp), reads=[b_gm], writes=[b_sc])
        op("dve", lambda e: e.scalar_tensor_tensor(g_, tmp, -1.0, g_, op0=ALU.mult, op1=ALU.mult), reads=[b_sc], writes=[b_sc])
        bkA, bbA = self.bank()
        op("pe", lambda e: [e.matmul(bkA[:, 0:64], lhsT=Tri, rhs=g_, start=True, stop=True),
                            e.matmul(bkA[:, 64:128], lhsT=onesf, rhs=g_, start=True, stop=True)],
           reads=[b_sc, b_gm], writes=[bbA])
        op("dve", lambda e: e.tensor_copy(gcum, bkA[:, 0:64]), reads=[bbA], writes=[b_sc])
        op("dve", lambda e: e.tensor_copy(gtot, bkA[:, 64:128]), reads=[bbA], writes=[b_sc])
        op("act", lambda e: e.activation(eg, gcum, AF.Exp), reads=[b_sc], writes=[b_sc])
        op("act", lambda e: e.activation(egt, gtot, AF.Exp), reads=[b_sc], writes=[b_sc])
        op("dve", lambda e: e.tensor_tensor(kds, gtot, gcum, op=ALU.subtract), reads=[b_sc], writes=[b_sc])
        op("act", lambda e: e.activation(kds, kds, AF.Exp), reads=[b_sc], writes=[b_sc])
        op("dve", lambda e: e.tensor_tensor(v3(bge), GT[:, :, 24:28], v3(eg), op=ALU.mult), reads=[b_sc, b_GT], writes=[b_sc])
        op("dve", lambda e: e.tensor_scalar_mul(v3(nb), GT[:, :, 24:28], -1.0), reads=[b_GT], writes=[b_sc])

        mark2 = self.off
        self.load_xT()
        xT, b_xT = self.xT, self.b_xT
        raw = [A([S + 3], F32) for _ in range(2)]; b_raw = [self.buf("raw%d" % i) for i in range(2)]
        cacc2 = [A([S], F32) for _ in range(2)]; b_cacc2 = [self.buf("cacc%d" % i) for i in range(2)]
        sq = A([S], BF16); b_sq = self.buf("sq")
        rs = [A([512], F32) for _ in range(2)]; b_rs = [self.buf("rs%d" % i) for i in range(2)]
        for i in range(2):
            op("dve", lambda e, i=i: e.memset(raw[i][:, 0:3], 0.0), writes=[b_raw[i]])
        for j in range(12):
            wj, bwj = self.ldw(w_in3[:, :, 1304 + j * 128:1304 + (j + 1) * 128], 128)
            rw, brw = raw[j % 2], b_raw[j % 2]
            cacc, b_cacc = cacc2[j % 2], b_cacc2[j % 2]
            self.proj_fm(wj, bwj, 0, 128, None, None,
                         evac=lambda tt, bk, bb, rw=rw, brw=brw: op(
                             "act", lambda e: e.copy(rw[:, 3 + tt * 512:3 + (tt + 1) * 512], bk[:, 0:512]), reads=[bb], writes=[brw]))
            op("act", lambda e, rw=rw, j=j, cacc=cacc: e.mul(cacc, rw[:, 3:3 + S], convw[:, j, 3:4]),
               reads=[brw, b_gm], writes=[b_cacc])
            for tap in (2, 1, 0):
                op("dve", lambda e, rw=rw, j=j, tap=tap, cacc=cacc: e.scalar_tensor_tensor(
                    cacc, rw[:, tap:tap + S], convw[:, j, tap:tap + 1], cacc, op0=ALU.mult, op1=ALU.add),
                   reads=[brw, b_gm, b_cacc], writes=[b_cacc])
            if j >= 8:
                op("act", lambda e, j=j, cacc=cacc: e.activation(QKV[:, j, :], cacc, AF.Silu), reads=[b_cacc], writes=[b_QKV[j]])
                continue
            op("act", lambda e, cacc=cacc: e.activation(cacc, cacc, AF.Silu), reads=[b_cacc], writes=[b_cacc])
            op("act", lambda e, cacc=cacc: e.activation(sq, cacc, AF.Square), reads=[b_cacc], writes=[b_sq])
            scale = (128.0 ** -0.5) if j < 4 else 1.0
            for tt in range(4):
                bk, bb = self.bank()
                r_, br_ = rs[tt % 2], b_rs[tt % 2]
                op("pe", lambda e, bk=bk, tt=tt: e.matmul(bk[:, 0:512], lhsT=onesb, rhs=sq[:, tt * 512:(tt + 1) * 512],
                                                          start=True, stop=True), reads=[b_sq, b_gm], writes=[bb])
                op("act", lambda e, bk=bk, r_=r_: e.activation(r_, bk[:, 0:512], AF.Sqrt, bias=1e-6), reads=[bb], writes=[br_])
                op("dve", lambda e, r_=r_: e.reciprocal(r_, r_), reads=[br_], writes=[br_])
                op("dve", lambda e, r_=r_, j=j, tt=tt, scale=scale, cacc=cacc: e.scalar_tensor_tensor(
                    QKV[:, j, tt * 512:(tt + 1) * 512], cacc[:, tt * 512:(tt + 1) * 512], scale, r_, op0=ALU.mult, op1=ALU.mult),
                   reads=[br_, b_cacc], writes=[b_QKV[j]])
        wg, bwg = self.ldw(w_in3[:, :, 2848:3360], 512)
        for t in range(NT):
            bk, bb = self.bank()
            op("pe", lambda e, bk=bk, t=t: [
                e.matmul(bk[:, 0:512], lhsT=xT[:, k, t * 128:(t + 1) * 128], rhs=wg[:, k, 0:512],
                         start=(k == 0), stop=(k == 7)) for k in range(8)], reads=[bwg] + b_xT, writes=[bb])
            op("act", lambda e, bk=bk, t=t: e.activation(GS[:, t, :], bk[:, 0:512], AF.Silu), reads=[bb], writes=[b_GS])
        self.barrier()
        self.off = mark2

        def set_bufs():
            d = {}
            for n in ("Pa", "Pb", "PTa", "PTb", "TTa", "TTb", "aqkT", "kbg", "kd", "vb", "negwT", "X21", "Tbd", "W21"):
                d[n] = A([4, 128], BF16)
            d["buf"] = self.buf("gset%d" % len(self.bufs))
            return d

        def early_bufs():
            d = {}
            for n in ("gsu", "ed", "edT", "dstr", "d21"):
                d[n] = A([4, 128], F32)
            d["buf"] = self.buf("gearly%d" % len(self.bufs))
            return d
        sets = [set_bufs() for _ in range(4)]
        earlys = [early_bufs() for _ in range(2)]
        Sf = A([4, 128], F32); Sb = A([4, 128], BF16); b_S = self.buf("gS")
        vnew = A([4, 128], BF16); avsb = A([4, 128], F32); of = A([4, 128], F32); osq = A([4, 128], F32)
        ytok = A([512], BF16); ssq = A([8], F32)
        b_w = self.buf("gwork")
        op("dve", lambda e: e.memset(Sf, 0.0), writes=[b_S])
        op("dve", lambda e: e.memset(Sb, 0.0), writes=[b_S])
        f2 = lambda a: a.rearrange("p h d -> p (h d)")

        def setup_steps(t, d, ea):
            bs = d["buf"]
            eb = ea["buf"]
            tok = slice(t * 128, (t + 1) * 128)
            qT = lambda h: QKV[:, h, tok]
            kT = lambda h: QKV[:, 4 + h, tok]
            vT = lambda h: QKV[:, 8 + h, tok]
            col = lambda a, h: a[:, t * 4 + h:t * 4 + h + 1]
            st = {}
            steps = []

            def s1():
                op("dve", lambda e: [e.tensor_scalar_mul(ea["gsu"][:, h, :], SU, col(g_, h)) for h in range(4)],
                   reads=[b_sc, b_gm], writes=[eb])
                bD, bbD = self.bank(); bDT, bbDT = self.bank()
                op("pe", lambda e: [e.matmul(bD[:, h * 128:(h + 1) * 128], lhsT=Tri, rhs=ea["gsu"][:, h, :], start=True, stop=True)
                                    for h in range(4)], reads=[eb, b_gm], writes=[bbD])
                op("pe", lambda e: [e.matmul(bDT[:, h * 128:(h + 1) * 128], lhsT=ea["gsu"][:, h, :], rhs=Tri, start=True, stop=True)
                                    for h in range(4)], reads=[eb, b_gm], writes=[bbDT])
                op("act", lambda e: e.activation(f2(ea["ed"]), bD[:, 0:512], AF.Exp), reads=[bbD], writes=[eb])
                op("act", lambda e: e.activation(f2(ea["edT"]), bDT[:, 0:512], AF.Exp), reads=[bbDT], writes=[eb])
                op("dve", lambda e: [e.tensor_tensor(ea["dstr"][:, h, :], ea["ed"][:, h, :], Mbd, op=ALU.mult) for h in range(4)],
                   reads=[eb, b_gm], writes=[eb])
                op("dve", lambda e: [e.tensor_tensor(ea["d21"][:, h, :], ea["ed"][:, h, :], M21, op=ALU.mult) for h in range(4)],
                   reads=[eb, b_gm], writes=[eb])
                op("dve", lambda e: [e.tensor_tensor(ea["edT"][:, h, :], ea["edT"][:, h, :], MTi, op=ALU.mult) for h in range(4)],
                   reads=[eb, b_gm], writes=[eb])
            steps.append(s1)

            def s2():
                bG, bbG = self.bank(); bA, bbA_ = self.bank()
                rq = b_QKV[0:8]
                op("pe", lambda e: [e.matmul(bG[:, h * 128:(h + 1) * 128], lhsT=kT(h), rhs=kT(h), start=True, stop=True)
                                    for h in range(4)], reads=rq, writes=[bbG])
                op("pe", lambda e: [e.matmul(bA[:, h * 128:(h + 1) * 128], lhsT=kT(h), rhs=qT(h), start=True, stop=True)
                                    for h in range(4)], reads=rq, writes=[bbA_])
                op("dve", lambda e: [e.scalar_tensor_tensor(d["Pa"][:, h, :], bG[:, h * 128:(h + 1) * 128], col(nb, h),
                                                            ea["dstr"][:, h, :], op0=ALU.mult, op1=ALU.mult) for h in range(4)],
                   reads=[bbG, eb, b_sc], writes=[bs])
                op("dve", lambda e: [e.scalar_tensor_tensor(d["X21"][:, h, :], bG[:, h * 128:(h + 1) * 128], col(nb, h),
                                                            ea["d21"][:, h, :], op0=ALU.mult, op1=ALU.mult) for h in range(4)],
                   reads=[bbG, eb, b_sc], writes=[bs])
                op("dve", lambda e: e.tensor_tensor(f2(d["aqkT"]), bA[:, 0:512], f2(ea["edT"]), op=ALU.mult),
                   reads=[bbA_, eb], writes=[bs])
                bT, bbT = self.bank()
                tv = bT.bitcast(BF16)
                op("pe", lambda e: [e.transpose(tv[:, h * 128:(h + 1) * 128], d["Pa"][:, h, :], ident) for h in range(4)],
                   reads=[bs, b_c], writes=[bbT])
                op("act", lambda e: e.copy(f2(d["PTa"]), tv[:, 0:512]), reads=[bbT], writes=[bs])
                op("dve", lambda e: e.tensor_tensor(f2(d["TTa"]), tv[:, 0:512], f2(ident4), op=ALU.add),
                   reads=[bbT, b_gm], writes=[bs])
                st["P"], st["PT"], st["TT"] = "Pa", "PTa", "TTa"
            steps.append(s2)

            def lvl(i):
                def f():
                    P, PT_, TT = d[st["P"]], d[st["PT"]], d[st["TT"]]
                    nP = "Pb" if st["P"] == "Pa" else "Pa"
                    nPT = "PTb" if st["PT"] == "PTa" else "PTa"
                    nTT = "TTb" if st["TT"] == "TTa" else "TTa"
                    bP, bbP = self.bank()
                    op("pe", lambda e: [e.matmul(bP[:, h * 128:(h + 1) * 128], lhsT=PT_[:, h, :], rhs=P[:, h, :], start=True, stop=True)
                                        for h in range(4)], reads=[bs], writes=[bbP])
                    if i < 5:
                        bPT, bbPT = self.bank()
                        op("pe", lambda e: [e.matmul(bPT[:, h * 128:(h + 1) * 128], lhsT=P[:, h, :], rhs=PT_[:, h, :],
                                                     start=True, stop=True) for h in range(4)], reads=[bs], writes=[bbPT])
                    op("act", lambda e: e.copy(f2(d[nP]), bP[:, 0:512]), reads=[bbP], writes=[bs])
                    if i < 5:
                        op("act", lambda e: e.copy(f2(d[nPT]), bPT[:, 0:512]), reads=[bbPT], writes=[bs])
                    bTT, bbTT = self.bank()
                    op("pe", lambda e: [e.matmul(bTT[:, h * 128:(h + 1) * 128], lhsT=d[nP][:, h, :], rhs=TT[:, h, :],
                                                 start=True, stop=True) for h in range(4)], reads=[bs], writes=[bbTT])
                    op("dve", lambda e: e.tensor_tensor(f2(d[nTT]), bTT[:, 0:512], f2(TT), op=ALU.add), reads=[bbTT, bs], writes=[bs])
                    st["P"], st["PT"], st["TT"] = nP, nPT, nTT
                return f
            for i in range(1, 6):
                steps.append(lvl(i))

            def s8():
                TT = d[st["TT"]]
                nTT = "TTb" if st["TT"] == "TTa" else "TTa"
                bT2, bbT2 = self.bank()
                t2 = bT2.bitcast(BF16)
                op("pe", lambda e: [e.transpose(t2[:, h * 128:(h + 1) * 128], TT[:, h, :], ident) for h in range(4)],
                   reads=[bs, b_c], writes=[bbT2])
                op("act", lambda e: e.copy(f2(d["Tbd"]), t2[:, 0:512]), reads=[bbT2], writes=[bs])
                bW2, bbW2 = self.bank()
                op("pe", lambda e: [e.matmul(bW2[:, h * 128:(h + 1) * 128], lhsT=d["X21"][:, h, :], rhs=TT[:, h, :], start=True, stop=True)
                                    for h in range(4)], reads=[bs], writes=[bbW2])
                op("act", lambda e: e.copy(f2(d["W21"]), bW2[:, 0:512]), reads=[bbW2], writes=[bs])
                bZ, bbZ = self.bank()
                op("pe", lambda e: [e.matmul(bZ[:, h * 128:(h + 1) * 128], lhsT=d["Tbd"][:, h, :], rhs=d["W21"][:, h, :], start=True, stop=True)
                                    for h in range(4)], reads=[bs], writes=[bbZ])
                op("dve", lambda e: e.tensor_tensor(f2(d[nTT]), bZ[:, 0:512], f2(TT), op=ALU.add), reads=[bbZ, bs], writes=[bs])
                st["TT"] = nTT
            steps.append(s8)

            def s9():
                bK, bbK = self.bank(); bV, bbV = self.bank()
                tk = bK.bitcast(BF16); tvv = bV.bitcast(BF16)
                op("pe", lambda e: [e.transpose(tk[:, h * 128:(h + 1) * 128], kT(h), ident) for h in range(4)],
                   reads=b_QKV[4:8] + [b_c], writes=[bbK])
                op("pe", lambda e: [e.transpose(tvv[:, h * 128:(h + 1) * 128], vT(h), ident) for h in range(4)],
                   reads=b_QKV[8:12] + [b_c], writes=[bbV])
                op("act", lambda e: [e.mul(d["kbg"][:, h, :], tk[:, h * 128:(h + 1) * 128], col(bge, h)) for h in range(4)],
                   reads=[bbK, b_sc], writes=[bs])
                op("dve", lambda e: [e.tensor_scalar_mul(d["kd"][:, h, :], tk[:, h * 128:(h + 1) * 128], col(kds, h)) for h in range(4)],
                   reads=[bbK, b_sc], writes=[bs])
                op("act", lambda e: [e.mul(d["vb"][:, h, :], tvv[:, h * 128:(h + 1) * 128], GT[:, t, 24 + h:25 + h])
                                     for h in range(4)], reads=[bbV, b_GT], writes=[bs])
                TT = d[st["TT"]]
                bW, bbW = self.bank()
                op("pe", lambda e: [e.matmul(bW[:, h * 128:(h + 1) * 128], lhsT=d["kbg"][:, h, :], rhs=TT[:, h, :], start=True, stop=True)
                                    for h in range(4)], reads=[bs], writes=[bbW])
                op("act", lambda e: e.mul(f2(d["negwT"]), bW[:, 0:512], -1.0), reads=[bbW], writes=[bs])
                st["TTfin"] = TT
            steps.append(s9)
            return steps, st

        def scan_steps(t, d, st):
            bs = d["buf"]
            tok = slice(t * 128, (t + 1) * 128)
            qT = lambda h: QKV[:, h, tok]
            col = lambda a, h: a[:, t * 4 + h:t * 4 + h + 1]
            b_o = self.b_gout
            sh = {}

            def p1():
                TT = st["TTfin"]
                bV, bbV = self.bank()
                op("pe", lambda e: [x for h in range(4) for x in (
                    e.matmul(bV[:, h * 128:(h + 1) * 128], lhsT=TT[:, h, :], rhs=d["vb"][:, h, :], start=True, stop=False),
                    e.matmul(bV[:, h * 128:(h + 1) * 128], lhsT=d["negwT"][:, h, :], rhs=Sb[:, h, :], start=False, stop=True))],
                   reads=[bs, b_S], writes=[bbV])
                op("act", lambda e: e.copy(f2(vnew), bV[:, 0:512]), reads=[bbV], writes=[b_w])

            def p2():
                (bQ, bbQ), (bAV, bbAV), (bS_, bbS) = self.banks[0], self.banks[1], self.banks[2]
                sh.update(bQ=bQ, bbQ=bbQ, bAV=bAV, bbAV=bbAV, bS_=bS_, bbS=bbS)
                op("pe", lambda e: [e.matmul(bQ[:, h * 128:(h + 1) * 128], lhsT=qT(h), rhs=Sb[:, h, :], start=True, stop=True)
                                    for h in range(4)], reads=b_QKV[0:4] + [b_S], writes=[bbQ])
                op("pe", lambda e: [e.matmul(bS_[:, h * 128:(h + 1) * 128], lhsT=d["kd"][:, h, :], rhs=vnew[:, h, :], start=True, stop=True)
                                    for h in range(4)], reads=[bs, b_w], writes=[bbS])
                op("pe", lambda e: [e.matmul(bAV[:, h * 128:(h + 1) * 128], lhsT=d["aqkT"][:, h, :], rhs=vnew[:, h, :], start=True, stop=True)
                                    for h in range(4)], reads=[bs, b_w], writes=[bbAV])

            def p3():
                bS_, bbS = sh["bS_"], sh["bbS"]
                op("dve", lambda e: [e.scalar_tensor_tensor(Sf[:, h, :], Sf[:, h, :], col(egt, h), bS_[:, h * 128:(h + 1) * 128],
                                                            op0=ALU.mult, op1=ALU.add) for h in range(4)],
                   reads=[bbS, b_S, b_sc], writes=[b_S])
                op("act", lambda e: e.copy(f2(Sb), f2(Sf)), reads=[b_S], writes=[b_S])

            def p4():
                bQ, bbQ, bAV, bbAV = sh["bQ"], sh["bbQ"], sh["bAV"], sh["bbAV"]
                op("act", lambda e: e.copy(f2(avsb), bAV[:, 0:512]), reads=[bbAV], writes=[b_o])
                op("dve", lambda e: [e.scalar_tensor_tensor(of[:, h, :], bQ[:, h * 128:(h + 1) * 128], col(eg, h), avsb[:, h, :],
                                                            op0=ALU.mult, op1=ALU.add) for h in range(4)],
                   reads=[bbQ, b_o, b_sc], writes=[b_o])

            def p5():
                op("act", lambda e: e.activation(f2(osq), f2(of), AF.Square), reads=[b_o], writes=[b_o])
                op("dve", lambda e: e.tensor_reduce(ssq[:, 0:4], osq, axis=AX.X, op=ALU.add), reads=[b_o], writes=[b_o])
                op("act", lambda e: e.activation(ssq[:, 0:4], ssq[:, 0:4], AF.Sqrt, bias=1e-6, scale=1.0 / 128.0), reads=[b_o], writes=[b_o])
                op("dve", lambda e: e.reciprocal(ssq[:, 4:8], ssq[:, 0:4]), reads=[b_o], writes=[b_o])
                op("dve", lambda e: [e.scalar_tensor_tensor(of[:, h, :], of[:, h, :], ssq[:, 4 + h:5 + h], normw[:, h * 128:(h + 1) * 128],
                                                            op0=ALU.mult, op1=ALU.mult) for h in range(4)],
                   reads=[b_o, b_gm], writes=[b_o])
                op("dve", lambda e: e.tensor_tensor(ytok, f2(of), GS[:, t, :], op=ALU.mult), reads=[b_o, b_GS], writes=[b_o])

            def p6():
                if self.dbg == "gdn":
                    op("dve", lambda e: e.tensor_copy(dbg_stg[0], ytok), reads=[b_o], writes=[dbg_stg[1]])
                    op("sp", lambda e: e.dma_start(out=self.dbg_d[t * 128:(t + 1) * 128, :], in_=dbg_stg[0]),
                       reads=[dbg_stg[1]], dma=1, key="dbgo")
                    return
                bY, bbY = self.bank()
                ty = bY.bitcast(BF16)
                op("pe", lambda e: [e.transpose(ty[:, c * 128:(c + 1) * 128], ytok[:, c * 128:(c + 1) * 128], ident) for c in range(4)],
                   reads=[b_o, b_c], writes=[bbY])
                op("act", lambda e: e.copy(ygT[:, :, tok], ty[:, 0:512].rearrange("p (c j) -> p c j", c=4)), reads=[bbY], writes=[b_ygT])
            return [p1, p2, p3, p4, p5, p6]

        self.b_gout = self.buf("gout")
        dbg_stg = None
        if self.dbg == "gdn":
            dbg_stg = (A([512], F32), self.buf("dstg"))

        self.banks_rot = self.banks[3:8]

        def zip_emit(streams):
            for k in range(max(len(x) for x in streams)):
                for x in streams:
                    if k < len(x):
                        x[k]()
        info = {}

        def mk_setup(t):
            steps, st = setup_steps(t, sets[t % 4], earlys[t % 2])
            info[t] = st
            return steps
        zip_emit([mk_setup(0), mk_setup(1)])
        for n in range(NT // 2):
            streams = []
            if n + 1 < NT // 2:
                streams += [mk_setup(2 * n + 2), mk_setup(2 * n + 3)]
            sc_ = []
            for t in (2 * n, 2 * n + 1):
                sc_ += scan_steps(t, sets[t % 4], info[t])
            streams.append(sc_)
            zip_emit(streams)

    def ln_tile(self, pre, gT, bT, out, b_pre, b_ln, tmp, st, b_out):
        op = self.fw.op
        op("dve", lambda e: e.tensor_reduce(st[:, 0:1], pre, axis=AX.X, op=ALU.add), reads=[b_pre], writes=[b_ln])
        op("dve", lambda e: e.tensor_scalar_mul(st[:, 1:2], st[:, 0:1], -1.0 / D), reads=[b_ln], writes=[b_ln])
        op("act", lambda e: e.activation(tmp, pre, AF.Square, bias=st[:, 1:2], accum_out=st[:, 2:3]),
           reads=[b_pre, b_ln], writes=[b_ln])
        op("act", lambda e: e.activation(st[:, 3:4], st[:, 2:3], AF.Sqrt, bias=1e-5, scale=1.0 / D), reads=[b_ln], writes=[b_ln])
        op("dve", lambda e: e.reciprocal(st[:, 4:5], st[:, 3:4]), reads=[b_ln], writes=[b_ln])
        op("dve", lambda e: e.tensor_tensor(st[:, 5:6], st[:, 1:2], st[:, 4:5], op=ALU.mult), reads=[b_ln], writes=[b_ln])
        op("act", lambda e: e.activation(tmp, pre, AF.Identity, bias=st[:, 5:6], scale=st[:, 4:5]), reads=[b_pre, b_ln], writes=[b_ln])
        op("dve", lambda e: e.tensor_tensor(tmp, tmp, gT, op=ALU.mult), reads=[b_ln, self.b_lnc], writes=[b_ln])
        op("dve", lambda e: e.tensor_tensor(out, tmp, bT, op=ALU.add), reads=[b_ln, self.b_lnc], writes=[b_out])

    def p34(self):
        nc, fw, op, A = self.nc, self.fw, self.fw.op, self.alloc
        ident, b_c = self.ident, self.b_c
        onsaT, b_onsaT, ygT, b_ygT = self.onsaT, self.b_onsaT, self.ygT, self.b_ygT
        x1T = self.oy
        din, w_in3 = self.din, self.w_in3
        self.banks_rot = self.banks
        x_d, out_d = din["x"], self.out_d
        wno_d = din["w_nsa_out"].rearrange("(k p) n -> p k n", p=128)
        wgo_d = din["w_gdn_out"].rearrange("(k p) n -> p k n", p=128)
        wo_d = din["w_o"].rearrange("(k p) n -> p k n", p=128)
        wup_d = din["ffn_w_up"].rearrange("(k p) n -> p k n", p=128)
        wdn_d = din["ffn_w_down"].rearrange("(j p) n -> p j n", p=128)
        x1 = A([NT, D], F32); b_x1 = [self.buf("x1_%d" % i) for i in range(NT)]
        lnc = A([2, D], F32); self.b_lnc = self.buf("lnc")
        lnt2 = [A([D], F32) for _ in range(2)]; lst2 = [A([8], F32) for _ in range(2)]
        b_ln2 = [self.buf("lnt%d" % i) for i in range(2)]
        self.wb = [(A([8, 512], BF16), self.buf("wb%d" % i)) for i in range(4)]
        self.wrot = 0
        mark3 = self.off
        for i, n in enumerate(("ln1_g", "ln1_b")):
            op("sp", lambda e, i=i, n=n: e.dma_start(out=lnc[:, i, :], in_=din[n]), writes=[self.b_lnc], dma=1, key="lnc%d" % i)
        wno = A([4, D], BF16); wgo = A([4, D], BF16); b_wout = self.buf("wout")
        op("pool", lambda e: e.dma_start(out=wno, in_=wno_d), writes=[b_wout], dma=1, key="wout0")
        op("pool", lambda e: e.dma_start(out=wgo, in_=wgo_d), writes=[b_wout], dma=1, key="wout1")
        xts = A([8, 512], BF16); b_xts = self.buf("xts")
        mixT = A([8, 512], BF16); b_mix = self.buf("mixT")
        sg = [A([512], F32) for _ in range(8)]; b_sg = [self.buf("sg%d" % i) for i in range(8)]
        xin = [A([D], F32) for _ in range(2)]; b_xin = [self.buf("xin%d" % i) for i in range(2)]
        x1b = A([D], BF16); b_x1b = self.buf("x1b")
        xn = [0]

        def wslot():
            i = self.wrot % 4
            self.wrot += 1
            return self.wb[i]

        for tt in range(4):
            tok = slice(tt * 512, (tt + 1) * 512)
            op("pool", lambda e, tt=tt: [e.dma_start(out=xts[:, k, :], in_=din["xT"][k * 128:(k + 1) * 128, tt * 512:(tt + 1) * 512])
                                         for k in range(8)], writes=[b_xts], dma=8)
            for cq in range(2):
                gwa, bgwa = wslot()
                gwb, bgwb = wslot()
                op("pool", lambda e, gwa=gwa, cq=cq: [e.dma_start(out=gwa[:, 0:4, :], in_=w_in3[:, 0:4, 3360 + cq * 512:3360 + (cq + 1) * 512]),
                                                      e.dma_start(out=gwa[:, 4:8, :], in_=w_in3[:, 4:8, 3360 + cq * 512:3360 + (cq + 1) * 512])],
                   writes=[bgwa], dma=2)
                op("pool", lambda e, gwb=gwb, cq=cq: [e.dma_start(out=gwb[:, 0:4, :], in_=w_in3[:, 0:4, 4384 + cq * 512:4384 + (cq + 1) * 512]),
                                                      e.dma_start(out=gwb[:, 4:8, :], in_=w_in3[:, 4:8, 4384 + cq * 512:4384 + (cq + 1) * 512])],
                   writes=[bgwb], dma=2)
                for c4 in range(4):
                    c = cq * 4 + c4
                    bga, bbga = self.bank(); bgb, bbgb = self.bank(); bya, bbya = self.bank(); byb, bbyb = self.bank()
                    op("pe", lambda e, gwa=gwa, bga=bga, c4=c4: [e.matmul(bga[:, 0:512], lhsT=gwa[:, k, c4 * 128:(c4 + 1) * 128], rhs=xts[:, k, :],
                                                                          start=(k == 0), stop=(k == 7)) for k in range(8)],
                       reads=[bgwa, b_xts], writes=[bbga])
                    op("pe", lambda e, gwb=gwb, bgb=bgb, c4=c4: [e.matmul(bgb[:, 0:512], lhsT=gwb[:, k, c4 * 128:(c4 + 1) * 128], rhs=xts[:, k, :],
                                                                          start=(k == 0), stop=(k == 7)) for k in range(8)],
                       reads=[bgwb, b_xts], writes=[bbgb])
                    op("pe", lambda e, c=c, bya=bya, tok=tok: [e.matmul(bya[:, 0:512], lhsT=wno[:, k, c * 128:(c + 1) * 128], rhs=onsaT[:, k, tok],
                                                                         start=(k == 0), stop=(k == 3)) for k in range(4)],
                       reads=[b_wout, b_onsaT], writes=[bbya])
                    op("pe", lambda e, c=c, byb=byb, tok=tok: [e.matmul(byb[:, 0:512], lhsT=wgo[:, k, c * 128:(c + 1) * 128], rhs=ygT[:, k, tok],
                                                                         start=(k == 0), stop=(k == 3)) for k in range(4)],
                       reads=[b_wout, b_ygT], writes=[bbyb])
                    o_ = (c % 2) * 4
                    op("act", lambda e, bga=bga, o_=o_: e.activation(sg[o_], bga[:, 0:512], AF.Sigmoid), reads=[bbga], writes=[b_sg[o_]])
                    op("act", lambda e, bgb=bgb, o_=o_: e.activation(sg[o_ + 1], bgb[:, 0:512], AF.Sigmoid), reads=[bbgb], writes=[b_sg[o_ + 1]])
                    op("dve", lambda e, bya=bya, o_=o_: e.tensor_tensor(sg[o_ + 2], bya[:, 0:512], sg[o_], op=ALU.mult),
                       reads=[bbya, b_sg[o_]], writes=[b_sg[o_ + 2]])
                    op("dve", lambda e, byb=byb, o_=o_: e.tensor_tensor(sg[o_ + 3], byb[:, 0:512], sg[o_ + 1], op=ALU.mult),
                       reads=[bbyb, b_sg[o_ + 1]], writes=[b_sg[o_ + 3]])
                    op("dve", lambda e, c=c, o_=o_: e.tensor_tensor(mixT[:, c, :], sg[o_ + 2], sg[o_ + 3], op=ALU.add),
                       reads=[b_sg[o_ + 2], b_sg[o_ + 3]], writes=[b_mix])
            wos = []
            for hf in range(2):
                wo_, bwo_ = wslot()
                op("pool", lambda e, wo_=wo_, hf=hf: [e.dma_start(out=wo_[:, 0:4, :], in_=wo_d[:, 0:4, hf * 512:(hf + 1) * 512]),
                                                      e.dma_start(out=wo_[:, 4:8, :], in_=wo_d[:, 4:8, hf * 512:(hf + 1) * 512])],
                   writes=[bwo_], dma=2)
                wos.append((wo_, bwo_))
            for s4 in range(4):
                t = tt * 4 + s4
                xi, bxi = xin[xn[0] % 2], b_xin[xn[0] % 2]
                xn[0] += 1
                op("sp", lambda e, xi=xi, t=t: e.dma_start(out=xi, in_=x_d[t * 128:(t + 1) * 128, :]), writes=[bxi], dma=1)
                for hf in range(2):
                    bk, bb = self.bank()
                    wo_, bwo_ = wos[hf]
                    op("pe", lambda e, bk=bk, wo_=wo_, s4=s4: [e.matmul(bk[:, 0:512], lhsT=mixT[:, k, s4 * 128:(s4 + 1) * 128], rhs=wo_[:, k, 0:512],
                                                                         start=(k == 0), stop=(k == 7)) for k in range(8)],
                       reads=[bwo_, b_mix], writes=[bb])
                    op("dve", lambda e, bk=bk, xi=xi, hf=hf: e.scalar_tensor_tensor(xi[:, hf * 512:(hf + 1) * 512], xi[:, hf * 512:(hf + 1) * 512],
                                                                                   ALPHA, bk[:, 0:512], op0=ALU.mult, op1=ALU.add),
                       reads=[bb, bxi], writes=[bxi])
                self.ln_tile(xi, lnc[:, 0, :], lnc[:, 1, :], x1[:, t, :], bxi, b_ln2[t % 2], lnt2[t % 2], lst2[t % 2], b_x1[t])
                op("act", lambda e, t=t: e.copy(x1b, x1[:, t, :]), reads=[b_x1[t]], writes=[b_x1b])
                bk, bb = self.bank()
                tv = bk.bitcast(BF16)
                op("pe", lambda e, tv=tv: [e.transpose(tv[:, k * 128:(k + 1) * 128], x1b[:, k * 128:(k + 1) * 128], ident) for k in range(8)],
                   reads=[b_x1b, b_c], writes=[bb])
                op("act", lambda e, tv=tv, t=t: e.copy(x1T[:, :, t * 128:(t + 1) * 128], tv[:, 0:1024].rearrange("p (k j) -> p k j", k=8)),
                   reads=[bb], writes=[b_onsaT, b_ygT])
        if self.dbg == "x1":
            for t in range(NT):
                op("sp", lambda e, t=t: e.dma_start(out=self.dbg_d[t * 128:(t + 1) * 128, :], in_=x1[:, t, 0:512]),
                   reads=[b_x1[t]], dma=1, key="dbgo")
            return
        self.barrier()
        self.off = mark3
        b_x1T = self.buf("x1T")
        for i, n in enumerate(("ln2_g", "ln2_b")):
            op("sp", lambda e, i=i, n=n: e.dma_start(out=lnc[:, i, :], in_=din[n]), writes=[self.b_lnc], dma=1, key="lnc%d" % i)
        fcw = A([44, 3], F32)
        op("sp", lambda e: e.dma_start(out=fcw, in_=din["ffn_convw"]), writes=[self.b_lnc], dma=1, key="lnc4")
        wdg = [A([4, D], BF16) for _ in range(2)]; b_wdg = [self.buf("wdg%d" % i) for i in range(2)]
        aTg = [A([4, 512], BF16) for _ in range(2)]; b_aTg = [self.buf("aTg%d" % i) for i in range(2)]
        hprev = A([44, 2], F32); b_hp = self.buf("hprev")
        raw = [A([514], F32) for _ in range(4)]; b_raw = [self.buf("fraw%d" % i) for i in range(4)]
        p0 = [A([512], F32) for _ in range(4)]; b_p0 = [self.buf("fp0%d" % i) for i in range(4)]
        cv = [A([512], F32) for _ in range(4)]; b_cv = [self.buf("fcv%d" % i) for i in range(4)]
        yout = [A([D], F32) for _ in range(2)]; b_yout = [self.buf("yout%d" % i) for i in range(2)]
        op("dve", lambda e: e.memset(hprev, 0.0), writes=[b_hp])
        an = [0]
        for cg in range(6):
            nch = 4 if cg < 5 else 2
            ch0 = cg * 4
            wua, bwua = wslot()
            wub, bwub = wslot()
            ncol = nch * 128
            op("pool", lambda e, wua=wua, ch0=ch0, ncol=ncol: [
                e.dma_start(out=wua[:, 0:4, 0:ncol], in_=wup_d[:, 0:4, ch0 * 128:ch0 * 128 + ncol]),
                e.dma_start(out=wua[:, 4:8, 0:ncol], in_=wup_d[:, 4:8, ch0 * 128:ch0 * 128 + ncol])], writes=[bwua], dma=2)
            op("pool", lambda e, wub=wub, ch0=ch0, ncol=ncol: [
                e.dma_start(out=wub[:, 0:4, 0:ncol], in_=wup_d[:, 0:4, FFN + ch0 * 128:FFN + ch0 * 128 + ncol]),
                e.dma_start(out=wub[:, 4:8, 0:ncol], in_=wup_d[:, 4:8, FFN + ch0 * 128:FFN + ch0 * 128 + ncol])], writes=[bwub], dma=2)
            wd_, bwd_ = wdg[cg % 2], b_wdg[cg % 2]
            op("pool", lambda e, wd_=wd_, ch0=ch0, nch=nch: e.dma_start(out=wd_[:, 0:nch, :], in_=wdn_d[:, ch0:ch0 + nch, :]),
               writes=[bwd_], dma=1)
            for tt in range(4):
                ag, bag = aTg[an[0] % 2], b_aTg[an[0] % 2]
                an[0] += 1
                for jj in range(nch):
                    for gv in range(2):
                        wu, bwu = (wua, bwua) if gv == 0 else (wub, bwub)
                        bk, bb = self.bank()
                        op("pe", lambda e, bk=bk, wu=wu, jj=jj, tt=tt: [
                            e.matmul(bk[:, 0:512], lhsT=wu[:, k, jj * 128:(jj + 1) * 128], rhs=x1T[:, k, tt * 512:(tt + 1) * 512],
                                     start=(k == 0), stop=(k == 7)) for k in range(8)], reads=[bwu, b_x1T], writes=[bb])
                        bi = (jj % 2) * 2 + gv
                        rw, brw = raw[bi], b_raw[bi]
                        ch = gv * 22 + ch0 + jj
                        op("act", lambda e, rw=rw, ch=ch: e.copy(rw[:, 0:2], hprev[:, ch, :]), reads=[b_hp], writes=[brw])
                        op("act", lambda e, rw=rw, bk=bk: e.copy(rw[:, 2:514], bk[:, 0:512]), reads=[bb], writes=[brw])
                        op("act", lambda e, rw=rw, ch=ch: e.copy(hprev[:, ch, :], rw[:, 512:514]), reads=[brw], writes=[b_hp])
                        p_, bp_ = p0[bi], b_p0[bi]
                        op("act", lambda e, rw=rw, p_=p_, ch=ch: e.mul(p_, rw[:, 0:512], fcw[:, ch, 0:1]), reads=[brw, self.b_lnc], writes=[bp_])
                        c_, bc_ = cv[bi], b_cv[bi]
                        op("dve", lambda e, rw=rw, c_=c_, p_=p_, ch=ch: e.scalar_tensor_tensor(
                            c_, rw[:, 1:513], fcw[:, ch, 1:2], p_, op0=ALU.mult, op1=ALU.add), reads=[brw, self.b_lnc, bp_], writes=[bc_])
                        op("dve", lambda e, rw=rw, c_=c_, ch=ch: e.scalar_tensor_tensor(
                            c_, rw[:, 2:514], fcw[:, ch, 2:3], c_, op0=ALU.mult, op1=ALU.add), reads=[brw, self.b_lnc, bc_], writes=[bc_])
                    cg_, cv_ = (jj % 2) * 2, (jj % 2) * 2 + 1
                    op("act", lambda e, cg_=cg_: e.activation(cv[cg_], cv[cg_], AF.Silu), reads=[b_cv[cg_]], writes=[b_cv[cg_]])
                    op("dve", lambda e, ag=ag, jj=jj, cg_=cg_, cv_=cv_: e.tensor_tensor(ag[:, jj, :], cv[cg_], cv[cv_], op=ALU.mult),
                       reads=[b_cv[cg_], b_cv[cv_]], writes=[bag])
                for s4 in range(4):
                    t = tt * 4 + s4
                    for hf in range(2):
                        bk, bb = self.bank()
                        op("pe", lambda e, bk=bk, ag=ag, wd_=wd_, s4=s4, hf=hf, nch=nch: [
                            e.matmul(bk[:, 0:512], lhsT=ag[:, jj, s4 * 128:(s4 + 1) * 128], rhs=wd_[:, jj, hf * 512:(hf + 1) * 512],
                                     start=(jj == 0), stop=(jj == nch - 1)) for jj in range(nch)], reads=[bag, bwd_], writes=[bb])
                        if cg == 0:
                            op("dve", lambda e, bk=bk, t=t, hf=hf: e.scalar_tensor_tensor(
                                x1[:, t, hf * 512:(hf + 1) * 512], x1[:, t, hf * 512:(hf + 1) * 512], ALPHA, bk[:, 0:512],
                                op0=ALU.mult, op1=ALU.add), reads=[bb, b_x1[t]], writes=[b_x1[t]])
                        else:
                            op("dve", lambda e, bk=bk, t=t, hf=hf: e.tensor_tensor(x1[:, t, hf * 512:(hf + 1) * 512], bk[:, 0:512],
                                                                                     x1[:, t, hf * 512:(hf + 1) * 512], op=ALU.add),
                               reads=[bb, b_x1[t]], writes=[b_x1[t]])
                    if cg == 5:
                        yo, byo = yout[t % 2], b_yout[t % 2]
                        self.ln_tile(x1[:, t, :], lnc[:, 0, :], lnc[:, 1, :], yo, b_x1[t], b_ln2[t % 2], lnt2[t % 2], lst2[t % 2], byo)
                        op("sp", lambda e, yo=yo, t=t: e.dma_start(out=out_d[t * 128:(t + 1) * 128, :], in_=yo), reads=[byo], dma=1,
                           key="outst%d" % (t % 2))


_PROG = {}


def _get_prog(dbg=""):
    if dbg not in _PROG:
        b = Bld(dbg)
        b.build()
        _PROG[dbg] = b
    return _PROG[dbg]


def _in_map(inputs, b, names):
    f = lambda a: np.ascontiguousarray(np.asarray(a, dtype=np.float32))
    x = np.asarray(inputs["x"])[b]
    m = {"xT": f(x.T), "x": f(x), "w_in": f(np.asarray(inputs["w_in"])[0])}
    for k, v in CONST.items():
        m["c_" + k] = v
    pos = np.asarray(inputs["nsa_cmp_pos"])[0].reshape(2, 16, 128)
    m["cmp_pos"] = f(pos.transpose(2, 0, 1))
    m["cmp_w1"] = f(np.asarray(inputs["nsa_cmp_w1"])[0])
    m["cmp_w2"] = f(np.asarray(inputs["nsa_cmp_w2"])[0])
    for n in ("ln1_g", "ln1_b", "ln2_g", "ln2_b"):
        m[n] = f(np.tile(np.asarray(inputs[n])[0][None, :], (128, 1)))
    m["ffn_convw"] = f(np.asarray(inputs["ffn_conv_w"])[0].reshape(3, 44, 128).transpose(2, 1, 0))
    for n in ("w_nsa_out", "w_gdn_out", "w_o", "ffn_w_up", "ffn_w_down"):
        m[n] = f(np.asarray(inputs[n])[0])
    m["gdn_normw"] = f(np.tile(np.asarray(inputs["gdn_norm_w"])[0][None, :], (128, 4)))
    m["gdn_convw"] = f(np.asarray(inputs["gdn_conv_w"])[0].reshape(4, 12, 128).transpose(2, 1, 0))
    m["gdn_dtb"] = f(np.tile(np.asarray(inputs["gdn_dt_bias"])[0][None, :], (128, 16)))
    m["gdn_alog"] = f(np.tile(np.asarray(inputs["gdn_a_log"])[0][None, :], (128, 16)))
    return {k: v for k, v in m.items() if k in names}


def run(inputs, dbg="", cores=8):
    bld = _get_prog(dbg)
    names = set(bld.din.keys())
    in_maps = [_in_map(inputs, b, names) for b in range(cores)]
    res = run_bass_kernel_spmd(bld.nc, in_maps, core_ids=list(range(cores)))
    return res


def kernel(**inputs):
    res = run(inputs, "", 8)
    out = np.stack([np.asarray(r["out"], dtype=np.float32) for r in res.results], axis=0)
    return out
```
